# Optimizing a Trainium2 kernel written in Bass

```python
import jax, jax.numpy as jnp
from jax import lax
import numpy as np

D_MODEL = 2048
BATCH = 16
SEQ = 2048
DEPTH = 1

CHUNK = 128
A_GROUPS = 8
A_GROUP_DIM = 128
A_WIDTH = A_GROUPS * A_GROUP_DIM
N_HEADS = 8
HEAD_DIM = 128
N_KV = 2
Q_WIDTH = N_HEADS * HEAD_DIM
KV_WIDTH = N_KV * HEAD_DIM
IDX_HEADS = 16
IDX_DIM = 64
TOPK_MAX = 256
Q_BLOCK = 128
D_FF = 5632
CONV_W = 3
EPS = 1e-6

kernel_name = "hybrid_gated_sgu_dsa_convffn_block"


def _col_sizes():
    return [A_WIDTH, A_WIDTH, Q_WIDTH, KV_WIDTH, KV_WIDTH,
            IDX_HEADS * IDX_DIM, IDX_DIM, IDX_HEADS, D_MODEL, D_MODEL]


def rms_norm(x, g):
    xf = x.astype(jnp.float32)
    y = xf * lax.rsqrt(jnp.mean(xf * xf, axis=-1, keepdims=True) + EPS)
    return (y * g.astype(jnp.float32)).astype(x.dtype)


def chunked_spatial_gating(u, v, v_norm_g, w_spatial, b_spatial):
    B, S, _ = v.shape
    n = S // CHUNK
    v = rms_norm(v, v_norm_g)
    vc = v.reshape(B, n, CHUNK, A_GROUPS, A_GROUP_DIM)
    causal = jnp.tril(jnp.ones((CHUNK, CHUNK), dtype=bool))
    w = jnp.where(causal[None], w_spatial, 0)
    mixed = jnp.einsum('gts,bnsgc->bntgc', w, vc) + b_spatial.T[None, None, :, :, None]
    return u * mixed.reshape(B, S, A_WIDTH)


def dsa_attention(q, k, v, q_idx, k_idx, w_idx):
    B, S = q.shape[0], q.shape[1]
    topk = min(TOPK_MAX, S // 4)
    n_blocks = S // Q_BLOCK
    rep = N_HEADS // N_KV
    key_pos = jnp.arange(S)
    k_idx_f = k_idx.astype(jnp.float32)
    idx_scale = IDX_DIM ** -0.5 * IDX_HEADS ** -0.5
    att_scale = HEAD_DIM ** -0.5

    def block(i):
        start = i * Q_BLOCK
        qb = lax.dynamic_slice_in_dim(q, start, Q_BLOCK, axis=1)
        qib = lax.dynamic_slice_in_dim(q_idx, start, Q_BLOCK, axis=1)
        wib = lax.dynamic_slice_in_dim(w_idx, start, Q_BLOCK, axis=1)
        q_pos = start + jnp.arange(Q_BLOCK)
        causal = key_pos[None, :] <= q_pos[:, None]
        logits = jnp.einsum('bthd,bsd->bths', qib.astype(jnp.float32), k_idx_f)
        score = jnp.einsum('bth,bths->bts', wib.astype(jnp.float32), jax.nn.relu(logits)) * idx_scale
        score = jnp.where(causal[None], score, -jnp.inf)
        top_val, top_idx = lax.top_k(score, topk)
        valid = top_val > -jnp.inf
        ks = jax.vmap(lambda kk, ii: kk[ii])(k, top_idx)
        vs = jax.vmap(lambda vv, ii: vv[ii])(v, top_idx)
        qg = qb.reshape(B, Q_BLOCK, N_KV, rep, HEAD_DIM)
        s = jnp.einsum('btgrd,btkgd->btgrk', qg, ks).astype(jnp.float32) * att_scale
        s = jnp.where(valid[:, :, None, None, :], s, -jnp.inf)
        p = jax.nn.softmax(s, axis=-1)
        o = jnp.einsum('btgrk,btkgd->btgrd', p.astype(vs.dtype), vs)
        return o.reshape(B, Q_BLOCK, Q_WIDTH)

    out = lax.map(block, jnp.arange(n_blocks))
    return out.transpose(1, 0, 2, 3).reshape(B, S, Q_WIDTH)


def causal_depthwise_conv(x, w, b):
    S = x.shape[1]
    xp = jnp.pad(x, ((0, 0), (CONV_W - 1, 0), (0, 0)))
    y = b + w[0] * xp[:, 0:S]
    for j in range(1, CONV_W):
        y = y + w[j] * xp[:, j:j + S]
    return y


def setup_inputs(seed: int = 0) -> dict:
    key = jax.random.key(seed)
    ks = jax.random.split(key, 20)
    n_cols = sum(_col_sizes())
    nrm = lambda k, shape, s: jax.random.normal(k, shape, jnp.float32) * s
    return {
        "x": nrm(ks[0], (BATCH, SEQ, D_MODEL), 1.0),
        "c": nrm(ks[1], (BATCH, D_MODEL), 1.0),
        "w_ada": nrm(ks[2], (D_MODEL, 6 * D_MODEL), 0.5 * D_MODEL ** -0.5),
        "b_ada": nrm(ks[3], (6 * D_MODEL,), 0.01),
        "norm1_g": 1.0 + nrm(ks[4], (D_MODEL,), 0.02),
        "w_in": nrm(ks[5], (D_MODEL, n_cols), D_MODEL ** -0.5),
        "v_norm_g": 1.0 + nrm(ks[6], (A_WIDTH,), 0.02),
        "w_spatial": nrm(ks[7], (A_GROUPS, CHUNK, CHUNK), CHUNK ** -0.5),
        "b_spatial": 1.0 + nrm(ks[8], (A_GROUPS, CHUNK), 0.1),
        "q_norm_g": 1.0 + nrm(ks[9], (HEAD_DIM,), 0.02),
        "k_norm_g": 1.0 + nrm(ks[10], (HEAD_DIM,), 0.02),
        "w_proj_a": nrm(ks[11], (A_WIDTH, D_MODEL), A_WIDTH ** -0.5),
        "w_proj_b": nrm(ks[12], (Q_WIDTH, D_MODEL), Q_WIDTH ** -0.5),
        "w_out": nrm(ks[13], (D_MODEL, D_MODEL), D_MODEL ** -0.5),
        "norm2_g": 1.0 + nrm(ks[14], (D_MODEL,), 0.02),
        "w_up": nrm(ks[15], (D_MODEL, 2 * D_FF), D_MODEL ** -0.5),
        "conv_w": nrm(ks[16], (CONV_W, 2 * D_FF), CONV_W ** -0.5),
        "conv_b": nrm(ks[17], (2 * D_FF,), 0.01),
        "w_down": nrm(ks[18], (D_FF, D_MODEL), D_FF ** -0.5),
    }


def reference(x, c, w_ada, b_ada, norm1_g, w_in, v_norm_g, w_spatial, b_spatial,
              q_norm_g, k_norm_g, w_proj_a, w_proj_b, w_out, norm2_g, w_up,
              conv_w, conv_b, w_down):
    B, S, _ = x.shape
    mod = (jax.nn.silu(c) @ w_ada + b_ada)[:, None, :]
    sh1, sc1, g1, sh2, sc2, g2 = jnp.split(mod, 6, axis=-1)
    split_pts = []
    acc = 0
    for n in _col_sizes()[:-1]:
        acc += n
        split_pts.append(acc)
    for _ in range(DEPTH):
        h = rms_norm(x, norm1_g) * (1.0 + sc1) + sh1
        proj = h @ w_in
        u, va, q, k, vb, qi, ki, wi, ga, gb = jnp.split(proj, split_pts, axis=-1)
        ya = chunked_spatial_gating(jax.nn.gelu(u), jax.nn.gelu(va), v_norm_g, w_spatial, b_spatial)
        qh = rms_norm(q.reshape(B, S, N_HEADS, HEAD_DIM), q_norm_g)
        kh = rms_norm(k.reshape(B, S, N_KV, HEAD_DIM), k_norm_g)
        vh = vb.reshape(B, S, N_KV, HEAD_DIM)
        yb = dsa_attention(qh, kh, vh, qi.reshape(B, S, IDX_HEADS, IDX_DIM), ki, wi)
        merged = jax.nn.sigmoid(ga) * (ya @ w_proj_a) + jax.nn.sigmoid(gb) * (yb @ w_proj_b)
        x = x + g1 * (merged @ w_out)
        h2 = rms_norm(x, norm2_g) * (1.0 + sc2) + sh2
        up = causal_depthwise_conv(h2 @ w_up, conv_w, conv_b)
        a, bval = jnp.split(up, 2, axis=-1)
        x = x + g2 * ((jax.nn.silu(a) * bval) @ w_down)
    return x
```

```python
import numpy as np
from contextlib import ExitStack
import concourse.bass as bass
import concourse.mybir as mybir
from concourse.bass_utils import run_bass_kernel_spmd

F32 = mybir.dt.float32
BF16 = mybir.dt.bfloat16
AF = mybir.ActivationFunctionType
ALU = mybir.AluOpType
AX = mybir.AxisListType

D = 2048
KC = D // 128
A_W = 1024
QW = 1024
KVW = 256
NIH = 16
IDD = 64
DFF = 5632
FC = DFF // 128
EPS = 1e-6
N_CORES = 8
ENGS = ("pe", "act", "dve", "pool", "sp")
MASK_ENG = "pool"
ACT_COUNT = True

C_U, C_VA, C_Q, C_K, C_VB, C_QI, C_KI, C_WI, C_GA, C_GB = 0, 1024, 2048, 3072, 3328, 3584, 4608, 4672, 4688, 6736
NCOLS_IN = 8784


class DSem:
    def __init__(self, h):
        self.h = h
        self.cnt = 0


class Buf:
    __slots__ = ("name", "w", "rs", "local")

    def __init__(self, name, local=True):
        self.name = name
        self.w = None
        self.rs = {}
        self.local = local


class Sched:
    def __init__(self, nc, es):
        self.nc = nc
        self.eng = {"pe": nc.tensor, "act": nc.scalar, "dve": nc.vector, "pool": nc.gpsimd, "sp": nc.sync}
        self.pending = []
        self.ecount = {e: 0 for e in ENGS}
        self.last_compute = {e: None for e in ENGS}
        self.signal = {e: set() for e in ENGS}
        self.rank = {e: {} for e in ENGS}
        self.sigc = {e: 0 for e in ENGS}
        self.flushed = {e: -1 for e in ENGS}
        self.flushed_sig = {e: None for e in ENGS}
        self.waited = {e: {} for e in ENGS}
        self.bar = {e: None for e in ENGS}
        self.sems = {e: es.enter_context(nc.semaphore("s_" + e)) for e in ENGS}
        self.es = es
        self.dsems = []
        self.n_wait = 0

    def dsem(self, name):
        d = DSem(self.es.enter_context(self.nc.semaphore("d_" + name)))
        self.dsems.append(d)
        return d

    def op(self, e, fn, reads=(), writes=(), dsem=None):
        need = {}

        def add(ev, kind):
            v = ev[2]
            if ev[0] == "e":
                if ev[1] == e and (e == "pe" or kind == "bar"):
                    return
                if v <= self.flushed[ev[1]] and v not in self.rank[ev[1]]:
                    v = self.flushed_sig[ev[1]]
            k = (ev[0], ev[1])
            if v > need.get(k, -1):
                need[k] = v

        touches_local = False
        for b in reads:
            touches_local |= b.local
            if b.w is not None:
                add(b.w, "raw")
        for b in writes:
            touches_local |= b.local
            if b.w is not None:
                add(b.w, "waw")
            for k, v in b.rs.items():
                add((k[0], k[1], v), "war")
        if self.bar[e] is not None and touches_local:
            for ev in self.bar[e]:
                add(ev, "bar")
            self.bar[e] = None
        idx = self.ecount[e]
        self.ecount[e] += 1
        final = []
        wd = self.waited[e]
        for k, v in need.items():
            if v <= wd.get(k, -1):
                continue
            wd[k] = v
            final.append((k[0], k[1], v))
            if k[0] == "e":
                self.signal[k[1]].add(v)
        if dsem is not None:
            dsem.cnt += 1
            me = ("d", dsem, 16 * dsem.cnt)
        else:
            me = ("e", e, idx)
            self.last_compute[e] = idx
        mk = (me[0], me[1])
        for b in reads:
            if b.rs.get(mk, -1) < me[2]:
                b.rs[mk] = me[2]
        for b in writes:
            b.w = me
            b.rs = {}
        self.pending.append((e, fn, final, dsem, idx))

    def barrier(self, skip=()):
        evs = []
        for e in ENGS:
            if self.last_compute[e] is not None:
                evs.append(("e", e, self.last_compute[e]))
                self.signal[e].add(self.last_compute[e])
        for d in self.dsems:
            if d.cnt > 0 and d not in skip:
                evs.append(("d", d, 16 * d.cnt))
        for e in ENGS:
            self.bar[e] = list(evs)
        self.flush()

    def flush(self):
        for e in ENGS:
            if self.last_compute[e] is not None:
                self.signal[e].add(self.last_compute[e])
        for (e, fn, deps, dsem, idx) in self.pending:
            g = self.eng[e]
            for ev in deps:
                self.n_wait += 1
                if ev[0] == "e":
                    g.wait_ge(self.sems[ev[1]], self.rank[ev[1]][ev[2]])
                else:
                    g.wait_ge(ev[1].h, ev[2])
            ins = fn(g)
            if dsem is not None:
                ins.then_inc(dsem.h, 16)
            elif idx in self.signal[e]:
                self.sigc[e] += 1
                self.rank[e][idx] = self.sigc[e]
                ins.then_inc(self.sems[e], 1)
        self.pending = []
        for e in ENGS:
            self.flushed[e] = self.ecount[e] - 1
            self.flushed_sig[e] = self.last_compute[e]


def build_nc(S=2048, NSEQ=2, MT=1024, NBIS=24, dbg=()):
    TOPK = min(256, S // 4)
    NMT = S // MT
    NT = MT // 128
    SUBW = min(512, MT)
    NSUB = MT // SUBW
    TPS = SUBW // 128
    NCH = S // 128
    nc = bass.Bass("TRN2", target_bir_lowering=False)
    dram = {}

    def din(name, shape):
        dram[name] = nc.dram_tensor(name, list(shape), F32, kind="ExternalInput").ap()
        return dram[name]

    x_d = din("x", [NSEQ * S, D])
    cst_d = din("cst", [128, CST_N(NBIS)])
    prm_d = din("prm", [128, PRM_N(NSEQ)])
    wspT_d = din("wspT", [128, 8 * 128])
    w_ada_d = din("w_ada", [D, 6 * D])
    w_in_d = din("w_in", [D, NCOLS_IN])
    w_pa_d = din("w_proj_a", [A_W, D])
    w_pb_d = din("w_proj_b", [QW, D])
    w_out_d = din("w_out", [D, D])
    w_up_d = din("w_up", [D, 2 * DFF])
    w_dn_d = din("w_down", [DFF, D])
    out_d = nc.dram_tensor("out", [NSEQ * S, D], F32, kind="ExternalOutput").ap()
    wsc_d = nc.dram_tensor("wscratch", [72, 128, 8192], BF16, kind="Internal").ap()
    dbg_d = {}
    for name, shape in dbg:
        dbg_d[name] = nc.dram_tensor("dbg_" + name, list(shape), F32, kind="ExternalOutput").ap()

    es = ExitStack()
    with es:
        sch = Sched(nc, es)
        op = sch.op

        sb_n = [0]

        def sb(name, shape, dt, stack=es):
            sb_n[0] += 1
            return stack.enter_context(nc.sbuf_tensor("%s_%d" % (name, sb_n[0]), list(shape), dt))

        cst = sb("cst", [128, CST_N(NBIS)], F32)
        prm = sb("prm", [128, PRM_N(NSEQ)], F32)
        identb = sb("identb", [128, 128], BF16)
        trib = sb("trib", [128, 128], BF16)
        onesb = sb("onesb", [128, 128], BF16)
        wspT = sb("wspTb", [128, 8, 128], BF16)
        modT = sb("modT", [128, 96, NSEQ], F32)
        gm = sb("gm", [128, NSEQ, 2, KC], F32)
        scT = sb("scT", [128, KC, NSEQ], BF16)
        epsT = sb("epsT", [128, 1], F32)
        kT = sb("kT", [128, 2, S], BF16)
        Vc = sb("Vc", [128, NCH, 2, 128], BF16)
        kiT = sb("kiT", [128, 2, S], BF16)
        halo = sb("halo", [128, KC, 2], BF16)
        uhx = sb("uhx", [128, 2 * FC, 4], F32)
        b_uh = [Buf("uh%d" % i, local=False) for i in range(2 * FC)]
        RING_E = 8192
        NRING = 2
        ring = [sb("ring%d" % i, [128, RING_E], BF16) for i in range(NRING)]
        ring_b = [Buf("ring%d" % i, local=False) for i in range(NRING)]
        ring_ds = [sch.dsem("ring%d" % i) for i in range(NRING)]
        ring_st = [sch.dsem("ringst%d" % i) for i in range(NRING)]
        ring_i = [0]
        P_ = {k: Buf(k, local=False) for k in ("cst", "prm", "identb", "trib", "onesb", "wspT", "modT", "gm", "scT",
                                               "epsT", "kT", "Vc", "kiT", "halo")}
        ds_misc = sch.dsem("misc")
        ds_x4 = sch.dsem("x4")
        ds_xo = sch.dsem("xo")
        ds_dbg = sch.dsem("dbg")
        psf = [es.enter_context(nc.psum_tensor("ps%d" % i, [128, 512], F32)) for i in range(8)]
        ps_b = [Buf("ps%d" % i, local=False) for i in range(8)]
        ps_i = [0]

        ps_held = set()

        def ps(hold=False):
            while True:
                i = ps_i[0] % 8
                ps_i[0] += 1
                if i not in ps_held:
                    break
            if hold:
                ps_held.add(i)
            return ps_b[i], psf[i]

        def ps_release(pb):
            ps_held.discard(ps_b.index(pb))

        ident_f = cst[:, 0:128]
        tri_f = cst[:, 128:256]
        negbig = cst[:, 256:384]
        ones_f = cst[:, 384:512]
        pow2 = cst[:, 512:512 + NBIS]
        PO = PRM_OFF(NSEQ)

        def pv(name, a=None, b=None):
            o, n = PO[name]
            if a is None:
                return prm[:, o:o + n]
            return prm[:, o + a:o + b]

        op("sp", lambda g: g.dma_start(out=cst[:], in_=cst_d[:, :]), writes=[P_["cst"]], dsem=sch.dsem("cst"))
        op("sp", lambda g: g.dma_start(out=prm[:], in_=prm_d[:, :]), writes=[P_["prm"]], dsem=sch.dsem("prm"))
        op("pool", lambda g: g.dma_start(out=wspT[:].rearrange("p g t -> p (g t)"), in_=wspT_d[:, :]),
           writes=[P_["wspT"]], dsem=ds_misc)
        op("dve", lambda g: g.tensor_copy(out=identb[:], in_=ident_f), reads=[P_["cst"]], writes=[P_["identb"]])
        op("dve", lambda g: g.tensor_copy(out=trib[:], in_=tri_f), reads=[P_["cst"]], writes=[P_["trib"]])
        op("dve", lambda g: g.tensor_copy(out=onesb[:], in_=ones_f), reads=[P_["cst"]], writes=[P_["onesb"]])
        op("dve", lambda g: g.memset(epsT[:], EPS), writes=[P_["epsT"]])
        op("dve", lambda g: g.memset(halo[:], 0.0), writes=[P_["halo"]])
        op("dve", lambda g: g.memset(kiT[:], 0.0), writes=[P_["kiT"]])
        op("dve", lambda g: g.memset(uhx[:], 0.0), writes=b_uh)
        op("dve", lambda g: g.tensor_tensor(out=wspT[:], in0=wspT[:],
                                            in1=trib[:].unsqueeze(1).to_broadcast([128, 8, 128]), op=ALU.mult),
           reads=[P_["wspT"], P_["trib"]], writes=[P_["wspT"]])
        op("act", lambda g: g.activation(out=scT[:].rearrange("p k b -> p (k b)"), in_=pv("cT"), func=AF.Silu),
           reads=[P_["prm"]], writes=[P_["scT"]])

        wsc_keys = {}
        WSC_MAX = 72

        def ring_load(parts, cache=True, nused=None):
            i = ring_i[0] % NRING
            ring_i[0] += 1
            key = tuple((p[2].tensor.name, int(p[2].offset), tuple(tuple(a) for a in p[2].ap)) for p in parts)
            if nused is None:
                nused = max(p[0] + p[1] for p in parts)
            if cache and key in wsc_keys:
                gid, b_sc_ = wsc_keys[key]
                op("pool", (lambda g: g.dma_start(out=ring[i][:, 0:nused], in_=wsc_d[gid, :, 0:nused])),
                   reads=[b_sc_], writes=[ring_b[i]], dsem=ring_ds[i])
                return ring_b[i], ring[i]
            for part in parts:
                off, n, src = part[0], part[1], part[2]
                if len(part) > 3:
                    dst = part[3](ring[i])
                else:
                    dst = ring[i][:, off:off + n]
                    if len(src.shape) == 3:
                        dst = dst.rearrange("p (k n) -> p k n", k=src.shape[1])
                op("pool", (lambda g, dst=dst, src=src: g.dma_start(out=dst, in_=src)),
                   writes=[ring_b[i]], dsem=ring_ds[i])
            if cache and len(wsc_keys) < WSC_MAX:
                gid = len(wsc_keys)
                b_sc_ = Buf("wsc%d" % gid, local=False)
                wsc_keys[key] = (gid, b_sc_)
                op("sp", (lambda g: g.dma_start(out=wsc_d[gid, :, 0:nused], in_=ring[i][:, 0:nused])),
                   reads=[ring_b[i]], writes=[b_sc_], dsem=ring_st[i])
            return ring_b[i], ring[i]

        def wsrc(w_d, k0, nk, c0, ncol):
            return w_d[k0 * 128:(k0 + nk) * 128, c0:c0 + ncol].rearrange("(k p) n -> p k n", p=128)

        def dbg_dump(name, src_ap, reads, row0=0, col0=0):
            if name not in dbg_d:
                return
            dd = dbg_d[name]
            shp = src_ap.shape
            op("pool", lambda g: g.dma_start(out=dd[row0:row0 + shp[0], col0:col0 + shp[1]], in_=src_ap),
               reads=reads + [Buf("dbgl")], dsem=ds_dbg)

        def emit_mod_group(grp):
            rb, rt = ring_load([(0, KC * 512, wsrc(w_ada_d, 0, KC, grp * 512, 512))], cache=False)
            wv = rt[:, 0:KC * 512].rearrange("p (k n) -> p k n", k=KC)
            for cj in range(4):
                j = grp * 4 + cj
                pb, pt = ps()
                for kc in range(KC):
                    op("pe", (lambda g, pt=pt, kc=kc, cj=cj, wv=wv:
                              g.matmul(pt[:, 0:NSEQ], wv[:, kc, cj * 128:(cj + 1) * 128], scT[:, kc, :],
                                       start=(kc == 0), stop=(kc == KC - 1))),
                       reads=[rb, P_["scT"]], writes=[pb])
                op("dve", (lambda g, pt=pt, j=j:
                           g.tensor_scalar(out=modT[:, j, :], in0=pt[:, 0:NSEQ], scalar1=pv("b_ada", j, j + 1),
                                           scalar2=None, op0=ALU.add)),
                   reads=[pb, P_["prm"]], writes=[P_["modT"]])

        def emit_gm(which):
            gname, joff = (("norm1_g", 16), ("norm2_g", 64))[which]
            for b in range(NSEQ):
                op("dve", (lambda g, b=b:
                           g.scalar_tensor_tensor(out=gm[:, b, which, :], in0=modT[:, joff:joff + 16, b], scalar=1.0,
                                                  in1=pv(gname), op0=ALU.add, op1=ALU.mult)),
                   reads=[P_["modT"], P_["prm"]], writes=[P_["gm"]])

        for grp in range(8):
            emit_mod_group(grp)
        emit_gm(0)
        mod_pending = list(range(8, 24))

        def emit_mod_pending(n):
            for _ in range(min(n, len(mod_pending))):
                emit_mod_group(mod_pending.pop(0))
            if not mod_pending and not mod_done[0]:
                emit_gm(1)
                mod_done[0] = True
        mod_done = [False]
        if "modT" in dbg_d:
            dbg_dump("modT", modT[:].rearrange("p j b -> p (j b)"), [P_["modT"]])

        SH1, SC1, G1, SH2, SC2, G2 = 0, 16, 32, 48, 64, 80

        def norm_tmp(stk):
            return (sb("n_ss", [128, 8], F32, stk), sb("n_junk", [128, D], BF16, stk),
                    sb("n_xs4", [128, TPS, D], BF16, stk), Buf("ss"), Buf("junk"), Buf("xs4"))

        def norm_to_hT(tmp, src4, src_b, which, bseq, hT, hT_b, sub, col_off):
            ss, junk, xs4, b_ss, b_junk, b_xs = tmp
            for tt in range(TPS):
                op("act", (lambda g, tt=tt: g.activation(out=junk[:], in_=src4[:, tt, :], func=AF.Square,
                                                         accum_out=ss[:, tt:tt + 1])),
                   reads=[src_b], writes=[b_junk, b_ss])
            op("act", lambda g: g.activation(out=ss[:, 0:TPS], in_=ss[:, 0:TPS], func=AF.Sqrt, scale=1.0 / D,
                                             bias=epsT[:, 0:1]),
               reads=[b_ss, P_["epsT"]], writes=[b_ss])
            op("dve", lambda g: g.reciprocal(out=ss[:, 0:TPS], in_=ss[:, 0:TPS]), reads=[b_ss], writes=[b_ss])
            for tt in range(TPS):
                op("dve", (lambda g, tt=tt: g.tensor_scalar(out=xs4[:, tt, :], in0=src4[:, tt, :],
                                                            scalar1=ss[:, tt:tt + 1], scalar2=None, op0=ALU.mult)),
                   reads=[src_b, b_ss], writes=[b_xs])
            for kc in range(KC):
                pb, pt = ps()
                ptb = pt.bitcast(BF16)
                for tt in range(TPS):
                    op("pe", (lambda g, ptb=ptb, tt=tt, kc=kc:
                              g.transpose(ptb[:, tt * 128:(tt + 1) * 128], xs4[:, tt, kc * 128:(kc + 1) * 128],
                                          identb[:])),
                       reads=[b_xs, P_["identb"]], writes=[pb])
                dst = hT[:, kc, col_off + sub * SUBW: col_off + (sub + 1) * SUBW]
                sc_ap = gm[:, bseq, which, kc:kc + 1]
                sh_ap = modT[:, (SH1 if which == 0 else SH2) + kc, bseq:bseq + 1]
                if kc % 2 == 0:
                    op("act", (lambda g, dst=dst, ptb=ptb, sc_ap=sc_ap, sh_ap=sh_ap:
                               g.activation(out=dst, in_=ptb[:, 0:SUBW], func=AF.Identity, scale=sc_ap, bias=sh_ap)),
                       reads=[pb, P_["gm"], P_["modT"]], writes=[hT_b[sub][kc]])
                else:
                    op("dve", (lambda g, dst=dst, ptb=ptb, sc_ap=sc_ap, sh_ap=sh_ap:
                               g.tensor_scalar(out=dst, in0=ptb[:, 0:SUBW], scalar1=sc_ap, scalar2=sh_ap,
                                               op0=ALU.mult, op1=ALU.add)),
                       reads=[pb, P_["gm"], P_["modT"]], writes=[hT_b[sub][kc]])

        def down_proj(stk, wd, KCN, act_of, act_reads, gsel, bseq, src_rows_ap, dst_rows_ap, src_dram_b, dst_dram_b,
                      after_sub=None, tag="", xo=None):
            if xo is None:
                xo = sb("dp_xo" + tag, [128, TPS, D], F32, stk)[:]
            b_xo = Buf("xo")
            yTs = [sb("dp_yT%d" % i + tag, [128, SUBW], F32, stk) for i in range(8 if KCN <= 16 else 4)]
            b_yTs = [Buf("yT%d" % i) for i in range(len(yTs))]
            for sub in range(NSUB):
                op("sp", (lambda g, sub=sub:
                          g.dma_start(out=xo, in_=src_rows_ap(sub).rearrange("(t p) d -> p t d", p=128))),
                   reads=[src_dram_b(sub)], writes=[b_xo], dsem=ds_xo)
                if KCN * 512 <= RING_E:
                    groups = [(f0, 512, [(0, KCN)]) for f0 in range(0, D, 512)]
                else:
                    kh = KCN // 2
                    groups = [(f0, 256, [(0, kh), (kh, KCN - kh)]) for f0 in range(0, D, 256)]
                fi = 0
                pending = []
                for (f0, ncol, ksplits) in groups:
                    nf = ncol // 128
                    pbs = [ps(hold=True) for _ in range(nf)]
                    for si, (k0, nk) in enumerate(ksplits):
                        rb, rt = ring_load([(0, nk * ncol, wsrc(wd, k0, nk, f0, ncol))])
                        wv = rt[:, 0:nk * ncol].rearrange("p (k n) -> p k n", k=nk)
                        for cj in range(nf):
                            pb, pt = pbs[cj]
                            for kk in range(nk):
                                kc = k0 + kk
                                op("pe", (lambda g, pt=pt, wv=wv, kk=kk, cj=cj, kc=kc, sub=sub:
                                          g.matmul(pt[:, 0:SUBW], wv[:, kk, cj * 128:(cj + 1) * 128], act_of(kc, sub),
                                                   start=(kc == 0), stop=(kc == KCN - 1))),
                                   reads=[rb] + act_reads(kc, sub), writes=[pb])
                    this = []
                    for cj in range(nf):
                        f = (f0 // 128) + cj
                        pb, pt = pbs[cj]
                        yi = fi % len(yTs)
                        fi += 1
                        yT, b_yT = yTs[yi], b_yTs[yi]
                        op("act", (lambda g, yT=yT, pt=pt, f=f:
                                   g.activation(out=yT[:], in_=pt[:, 0:SUBW], func=AF.Copy,
                                                scale=modT[:, gsel + f, bseq:bseq + 1])),
                           reads=[pb, P_["modT"]], writes=[b_yT])
                        ps_release(pb)

                        def tr(yT=yT, b_yT=b_yT, f=f):
                            pb2, pt2 = ps()
                            for tt in range(TPS):
                                op("pe", (lambda g, tt=tt:
                                          g.transpose(pt2[:, tt * 128:(tt + 1) * 128], yT[:, tt * 128:(tt + 1) * 128],
                                                      ident_f)),
                                   reads=[b_yT, P_["cst"]], writes=[pb2])
                            op("dve", (lambda g:
                                       g.tensor_tensor(out=xo[:, :, f * 128:(f + 1) * 128],
                                                       in0=xo[:, :, f * 128:(f + 1) * 128],
                                                       in1=pt2[:, 0:SUBW].rearrange("p (t c) -> p t c", t=TPS),
                                                       op=ALU.add)),
                               reads=[pb2, b_xo], writes=[b_xo])
                        this.append(tr)
                    for t_ in pending:
                        t_()
                    pending = this
                for t_ in pending:
                    t_()
                op("sp", (lambda g, sub=sub:
                          g.dma_start(out=dst_rows_ap(sub).rearrange("(t p) d -> p t d", p=128), in_=xo)),
                   reads=[b_xo], writes=[dst_dram_b(sub)], dsem=ds_xo)
                if after_sub is not None:
                    after_sub(sub, xo, b_xo)

        def gemm_F(rb, wv, ncj, KCN, act_of, act_reads, evac, cj0=0):
            for cj in range(ncj):
                for sub in range(NSUB):
                    pb, pt = ps()
                    for kc in range(KCN):
                        op("pe", (lambda g, pt=pt, wv=wv, kc=kc, cj=cj, sub=sub:
                                  g.matmul(pt[:, 0:SUBW], wv[:, kc, (cj0 + cj) * 128:(cj0 + cj + 1) * 128],
                                           act_of(kc, sub), start=(kc == 0), stop=(kc == KCN - 1))),
                           reads=[rb] + act_reads(kc, sub), writes=[pb])
                    evac(cj, sub, pb, pt)

        dram_out_b = {}

        def dob(seq, m, sub):
            k = (seq, m, sub)
            if k not in dram_out_b:
                dram_out_b[k] = Buf("dout%d_%d_%d" % k, local=False)
            return dram_out_b[k]

        dram_x_b = Buf("xin", local=False)
        att_scale = 128.0 ** -0.5

        for seq in range(NSEQ):
            for m in range(NMT):
                tok0 = seq * S + m * MT
                mt_es = ExitStack()
                with mt_es:
                    hTflat = sb("hT", [128, KC * (MT + 2)], BF16, mt_es)
                    hT = hTflat[:].rearrange("p (k n) -> p k n", k=KC)
                    xo8 = None
                    if KC * (MT + 2) // 2 >= TPS * D:
                        xo8 = hTflat.bitcast(F32)[:, 0:TPS * D].rearrange("p (t d) -> p t d", t=TPS)
                    hT_b = [[Buf("hT%d_%d" % (s_, k)) for k in range(KC)] for s_ in range(NSUB)]
                    hT_halo_b = Buf("hThalo")
                    mg_es = ExitStack()
                    mg = sb("merged", [128, KC, MT], BF16, mg_es)
                    b_mg = [[Buf("mg%d_%d" % (f_, s_)) for s_ in range(NSUB)] for f_ in range(KC)]

                    def h_of(kc, sub, hT=hT):
                        return hT[:, kc, 2 + sub * SUBW: 2 + (sub + 1) * SUBW]

                    def h_reads(kc, sub, hT_b=hT_b):
                        return [hT_b[sub][kc]]

                    p_es = ExitStack()
                    with p_es:
                        x4 = sb("x4", [128, TPS, D], F32, p_es)
                        b_x4 = Buf("x4")
                        for sub in range(NSUB):
                            r0 = tok0 + sub * SUBW
                            op("sp", (lambda g, r0=r0: g.dma_start(
                                out=x4[:], in_=x_d[r0:r0 + SUBW, :].rearrange("(t p) d -> p t d", p=128))),
                               reads=[dram_x_b], writes=[b_x4], dsem=ds_x4)
                            if sub == 0:
                                ntmp = norm_tmp(p_es)
                            norm_to_hT(ntmp, x4, b_x4, 0, seq, hT, hT_b, sub, 2)
                        sch.barrier(skip=ring_ds + ring_st)
                    if "hT" in dbg_d and seq == 0:
                        for kc in range(KC):
                            for sub in range(NSUB):
                                dbg_dump("hT", h_of(kc, sub), [hT_b[sub][kc]], row0=kc * 128,
                                         col0=m * MT + sub * SUBW)

                    att_es = ExitStack()
                    with att_es:
                        qT = sb("qT", [128, NT, 8, 128], BF16, att_es)
                        b_qT = [[Buf("qT%d_%d" % (t_, h_)) for h_ in range(8)] for t_ in range(NT)]
                        qiT = sb("qiT", [128, 8, MT], BF16, att_es)
                        b_qiT = [[Buf("qiT%d_%d" % (p_, s_)) for s_ in range(NSUB)] for p_ in range(8)]
                        absw = sb("absw", [128, NT, NIH], F32, att_es)
                        sgnw = sb("sgnw", [128, NT, NIH], F32, att_es)
                        b_w = [Buf("widx%d" % t_) for t_ in range(NT)]
                        p1_es = ExitStack()
                        with p1_es:
                            sq = [sb("p1_sq%d" % i, [128, SUBW], F32, p1_es) for i in range(2)]
                            b_sq = [Buf("sq%d" % i) for i in range(2)]
                            rr = [sb("p1_rr%d" % i, [128, SUBW], F32, p1_es) for i in range(2)]
                            b_rr = [Buf("rr%d" % i) for i in range(2)]
                            cnt = [0]

                            qk_def = []

                            def qk_flush():
                                while qk_def:
                                    qk_def.pop(0)()

                            def qk_evac(pb, pt, gname, dst_ap, dst_bufs):
                                qk_flush()
                                i = cnt[0] % 2
                                cnt[0] += 1
                                ps_held.add(ps_b.index(pb))
                                op("act", (lambda g: g.activation(out=sq[i][:], in_=pt[:, 0:SUBW], func=AF.Square)),
                                   reads=[pb], writes=[b_sq[i]])

                                def part_b():
                                    pb2, pt2 = ps()
                                    op("pe", (lambda g: g.matmul(pt2[:, 0:SUBW], ones_f, sq[i][:], start=True, stop=True)),
                                       reads=[b_sq[i], P_["cst"]], writes=[pb2])
                                    op("act", (lambda g: g.activation(out=rr[i][:], in_=pt2[:, 0:SUBW], func=AF.Sqrt,
                                                                      scale=1.0 / 128, bias=epsT[:, 0:1])),
                                       reads=[pb2, P_["epsT"]], writes=[b_rr[i]])
                                    op("dve", (lambda g: g.reciprocal(out=rr[i][:], in_=rr[i][:])),
                                       reads=[b_rr[i]], writes=[b_rr[i]])
                                    op("dve", (lambda g: g.scalar_tensor_tensor(
                                        out=dst_ap, in0=pt[:, 0:SUBW].rearrange("p (t c) -> p t c", t=TPS)
                                        if len(dst_ap.shape) == 3 else pt[:, 0:SUBW],
                                        scalar=pv(gname), in1=rr[i][:].rearrange("p (t c) -> p t c", t=TPS)
                                        if len(dst_ap.shape) == 3 else rr[i][:], op0=ALU.mult, op1=ALU.mult)),
                                       reads=[pb, b_rr[i], P_["prm"]], writes=dst_bufs)
                                    ps_release(pb)
                                qk_def.append(part_b)

                            for qg in range(2):
                                rb, rt = ring_load([(0, KC * 512, wsrc(w_in_d, 0, KC, C_Q + qg * 512, 512))])
                                wv = rt[:, 0:KC * 512].rearrange("p (k n) -> p k n", k=KC)

                                def ev_q(cj, sub, pb, pt, qg=qg):
                                    h = qg * 4 + cj
                                    dst = qT[:, sub * TPS:(sub + 1) * TPS, h, :]
                                    qk_evac(pb, pt, "q_norm_g", dst, [b_qT[sub * TPS + t_][h] for t_ in range(TPS)])
                                gemm_F(rb, wv, 4, KC, h_of, h_reads, ev_q)
                            rb, rt = ring_load([(0, KC * 512, wsrc(w_in_d, 0, KC, C_K, 512))])
                            wv = rt[:, 0:KC * 512].rearrange("p (k n) -> p k n", k=KC)

                            def ev_k(cj, sub, pb, pt):
                                c0 = m * MT + sub * SUBW
                                qk_evac(pb, pt, "k_norm_g", kT[:, cj, c0:c0 + SUBW], [P_["kT"]])
                            gemm_F(rb, wv, 2, KC, h_of, h_reads, ev_k)
                            qk_flush()
                            for tt in range(NT):
                                pb, pt = ps()
                                sub, ti = divmod(tt, TPS)
                                for kc in range(KC):
                                    op("pe", (lambda g, pt=pt, kc=kc, sub=sub, ti=ti, wv=wv:
                                              g.matmul(pt[:, 0:256], hT[:, kc, 2 + sub * SUBW + ti * 128:
                                                                        2 + sub * SUBW + (ti + 1) * 128],
                                                       wv[:, kc, 256:512], start=(kc == 0), stop=(kc == KC - 1))),
                                       reads=[rb, hT_b[sub][kc]], writes=[pb])
                                ch = m * NT + tt
                                op("act", (lambda g, pt=pt, ch=ch:
                                           g.activation(out=Vc[:, ch, :, :].rearrange("p g d -> p (g d)"),
                                                        in_=pt[:, 0:256], func=AF.Copy)),
                                   reads=[pb], writes=[P_["Vc"]])
                            for qg in range(2):
                                rb, rt = ring_load([(0, KC * 512, wsrc(w_in_d, 0, KC, C_QI + qg * 512, 512))])
                                wv = rt[:, 0:KC * 512].rearrange("p (k n) -> p k n", k=KC)

                                def ev_qi(cj, sub, pb, pt, qg=qg):
                                    pr = qg * 4 + cj
                                    op("dve", (lambda g: g.tensor_copy(out=qiT[:, pr, sub * SUBW:(sub + 1) * SUBW],
                                                                       in_=pt[:, 0:SUBW])),
                                       reads=[pb], writes=[b_qiT[pr][sub]])
                                gemm_F(rb, wv, 4, KC, h_of, h_reads, ev_qi)
                            kdst = lambda half: (lambda rt_: rt_[:, 0:KC * 128].rearrange("p (k n) -> p k n", k=KC)
                                                 [:, :, half * 64:(half + 1) * 64])
                            rb, rt = ring_load([(0, 0, wsrc(w_in_d, 0, KC, C_KI, 64), kdst(0)),
                                                (0, 0, wsrc(w_in_d, 0, KC, C_KI, 64), kdst(1)),
                                                (KC * 128, KC * 16, wsrc(w_in_d, 0, KC, C_WI, 16))], nused=KC * 144)
                            wkk = rt[:, 0:KC * 128].rearrange("p (k n) -> p k n", k=KC)
                            wwi = rt[:, KC * 128:KC * 144].rearrange("p (k n) -> p k n", k=KC)
                            for sub in range(NSUB):
                                pb, pt = ps()
                                for kc in range(KC):
                                    op("pe", (lambda g, pt=pt, kc=kc, sub=sub:
                                              g.matmul(pt[:, 0:SUBW], wkk[:, kc, :], h_of(kc, sub),
                                                       start=(kc == 0), stop=(kc == KC - 1))),
                                       reads=[rb, hT_b[sub][kc]], writes=[pb])
                                c0 = m * MT + sub * SUBW
                                for half in range(2):
                                    op("act", (lambda g, pt=pt, c0=c0, half=half:
                                               g.activation(out=kiT[half * 64:(half + 1) * 64, half, c0:c0 + SUBW],
                                                            in_=pt[half * 64:(half + 1) * 64, 0:SUBW], func=AF.Copy)),
                                       reads=[pb], writes=[P_["kiT"]])
                            for tt in range(NT):
                                pb, pt = ps()
                                sub, ti = divmod(tt, TPS)
                                for kc in range(KC):
                                    op("pe", (lambda g, pt=pt, kc=kc, sub=sub, ti=ti:
                                              g.matmul(pt[:, 0:NIH], hT[:, kc, 2 + sub * SUBW + ti * 128:
                                                                        2 + sub * SUBW + (ti + 1) * 128],
                                                       wwi[:, kc, :], start=(kc == 0), stop=(kc == KC - 1))),
                                       reads=[rb, hT_b[sub][kc]], writes=[pb])
                                op("act", (lambda g, pt=pt, tt=tt:
                                           g.activation(out=absw[:, tt, :], in_=pt[:, 0:NIH], func=AF.Abs)),
                                   reads=[pb], writes=[b_w[tt]])
                                op("act", (lambda g, pt=pt, tt=tt:
                                           g.activation(out=sgnw[:, tt, :], in_=pt[:, 0:NIH], func=AF.Sign)),
                                   reads=[pb], writes=[b_w[tt]])
                            sch.barrier(skip=ring_ds + ring_st)
                        if seq == 0:
                            if "qT" in dbg_d:
                                for tt in range(NT):
                                    dbg_dump("qT", qT[:, tt, :, :].rearrange("p h t -> p (h t)"), b_qT[tt],
                                             col0=(m * NT + tt) * 1024)
                            if "qiT" in dbg_d:
                                for pr in range(8):
                                    for sub in range(NSUB):
                                        dbg_dump("qiT", qiT[:, pr, sub * SUBW:(sub + 1) * SUBW], [b_qiT[pr][sub]],
                                                 row0=pr * 128, col0=m * MT + sub * SUBW)
                            if "absw" in dbg_d and m == 0:
                                dbg_dump("absw", absw[:].rearrange("p t h -> p (t h)"), b_w)

                        p2_es = ExitStack()
                        with p2_es:
                            sc = [sb("sc%d" % i, [128, S], F32, p2_es) for i in range(2)]
                            b_sc = [Buf("sc%d" % i) for i in range(2)]
                            Rr = [sb("Rr%d" % i, [128, 512], BF16, p2_es) for i in range(3)]
                            b_Rr = [Buf("Rr%d" % i) for i in range(3)]
                            Dg = sb("Dg", [128, NIH, 128], BF16, p2_es)
                            b_Dg = Buf("Dg")
                            mTS = sb("mTS", [128, S], BF16, p2_es)
                            b_mTS = Buf("mTS")
                            junkb, b_junkb = mTS, b_mTS
                            mdbg = sb("mdbg", [128, S], F32, p2_es) if "mTS" in dbg_d else None
                            b_mdbg = Buf("mdbg")
                            mT = [sb("mT%d" % i, [128, NCH, 128], BF16, p2_es) for i in range(2)]
                            b_mT = [Buf("mT%d" % i) for i in range(2)]
                            pT = [sb("pT%d" % i, [128, 512], BF16, p2_es) for i in range(4)]
                            b_pT = [Buf("pT%d" % i) for i in range(4)]
                            rinv = sb("rinv", [128, 512], F32, p2_es)
                            b_rinv = Buf("rinv")
                            bs = sb("bs", [128, 8], F32, p2_es)
                            b_bs = Buf("bs")
                            wtab = sb("wtab", [128, NBIS], F32, p2_es)
                            nwtab = sb("nwtab", [128, NBIS], F32, p2_es)
                            b_wtab = Buf("wtab")
                            b_bs6 = Buf("bs6")

                            ctr = {"ri": 0, "pi": 0}
                            if seq == 0 and m == 0:
                                build_nc.p2_slack = nc.sbuf_bytes_remaining

                            def tile_info(tl):
                                qg_ = m * NT + tl
                                nk = (qg_ + 1) * 128
                                return qg_, nk, (nk > TOPK)

                            def score_steps(tl):
                                qg_, nk, _ = tile_info(tl)
                                scb, b_scb = sc[tl % 2], b_sc[tl % 2]
                                steps = []

                                def build_dg():
                                    for h in range(NIH):
                                        op("dve", (lambda g, h=h:
                                                   g.tensor_scalar(out=Dg[:, h, :], in0=identb[:],
                                                                   scalar1=sgnw[:, tl, h:h + 1], scalar2=None, op0=ALU.mult)),
                                           reads=[P_["identb"], b_w[tl]], writes=[b_Dg])
                                steps.append(build_dg)
                                nkb = (nk + 511) // 512
                                for kb in range(nkb):
                                    ncol = min(512, nk - kb * 512)
                                    stt = {}

                                    def front(h, kb=kb, ncol=ncol, stt=stt):
                                        pr, half = divmod(h, 2)
                                        pb, pt = ps()
                                        op("pe", (lambda g: g.matmul(pt[:, 0:ncol], qiT[:, pr, tl * 128:(tl + 1) * 128],
                                                                     kiT[:, half, kb * 512:kb * 512 + ncol],
                                                                     start=True, stop=True)),
                                           reads=[b_qiT[pr][tl // TPS], P_["kiT"]], writes=[pb])
                                        r_i = ctr["ri"] % 3
                                        ctr["ri"] += 1
                                        if h % 3 == 2:
                                            op("dve", (lambda g: g.tensor_scalar(out=Rr[r_i][:, 0:ncol], in0=pt[:, 0:ncol],
                                                                                 scalar1=absw[:, tl, h:h + 1], scalar2=0.0,
                                                                                 op0=ALU.mult, op1=ALU.max)),
                                               reads=[pb, b_w[tl]], writes=[b_Rr[r_i]])
                                        else:
                                            op("act", (lambda g: g.activation(out=Rr[r_i][:, 0:ncol], in_=pt[:, 0:ncol],
                                                                              func=AF.Relu, scale=absw[:, tl, h:h + 1])),
                                               reads=[pb, b_w[tl]], writes=[b_Rr[r_i]])
                                        stt[h] = r_i

                                    def back(h, ncol=ncol, stt=stt):
                                        r_i = stt.pop(h)
                                        pba, pta = stt["acc"]
                                        op("pe", (lambda g: g.matmul(pta[:, 0:ncol], Dg[:, h, :], Rr[r_i][:, 0:ncol],
                                                                     start=(h == 0), stop=(h == NIH - 1))),
                                           reads=[b_Dg, b_Rr[r_i]], writes=[pba])
                                    LA_ = 2

                                    def mk(h, kb=kb, ncol=ncol, stt=stt, front=front, back=back):
                                        def f():
                                            if h == 0:
                                                stt["acc"] = ps(hold=True)
                                                for k_ in range(LA_):
                                                    front(k_)
                                            if h + LA_ < NIH:
                                                front(h + LA_)
                                            back(h)
                                            if h == NIH - 1:
                                                pba, pta = stt["acc"]
                                                dst = scb[:, kb * 512:kb * 512 + ncol]
                                                op("act", (lambda g: g.activation(out=dst, in_=pta[:, 0:ncol], func=AF.Copy)),
                                                   reads=[pba], writes=[b_scb])
                                                ps_release(pba)
                                                if kb == nkb - 1:
                                                    dg = scb[:, qg_ * 128:(qg_ + 1) * 128]
                                                    op("dve", (lambda g: g.tensor_tensor(out=dg, in0=dg, in1=negbig,
                                                                                         op=ALU.add)),
                                                       reads=[b_scb, P_["cst"]], writes=[b_scb])
                                                    if "sc" in dbg_d and seq == 0:
                                                        dbg_dump("sc", scb[:, 0:nk], [b_scb], row0=qg_ * 128)
                                        return f
                                    for h in range(NIH):
                                        steps.append(mk(h))
                                return steps

                            def emit_bisect_init(tl):
                                qg_, nk, _ = tile_info(tl)
                                scb, b_scb = sc[tl % 2], b_sc[tl % 2]
                                scv = scb[:, 0:nk]
                                op("dve", (lambda g: g.tensor_reduce(out=bs[:, 0:1], in_=scb[:, 0:TOPK],
                                                                     axis=AX.X, op=ALU.min)),
                                   reads=[b_scb], writes=[b_bs])
                                op("dve", (lambda g: g.tensor_reduce(out=bs[:, 4:5], in_=scv, axis=AX.X, op=ALU.max)),
                                   reads=[b_scb], writes=[b_bs])
                                op("dve", (lambda g: g.tensor_tensor(out=bs[:, 5:6], in0=bs[:, 4:5], in1=bs[:, 0:1],
                                                                     op=ALU.subtract)),
                                   reads=[b_bs], writes=[b_bs])
                                op("dve", (lambda g: g.tensor_scalar(out=wtab[:], in0=pow2, scalar1=bs[:, 5:6],
                                                                     scalar2=None, op0=ALU.mult)),
                                   reads=[b_bs, P_["cst"]], writes=[b_wtab])
                                op("dve", (lambda g: g.tensor_scalar(out=nwtab[:], in0=pow2, scalar1=bs[:, 5:6],
                                                                     scalar2=-1.0, op0=ALU.mult, op1=ALU.mult)),
                                   reads=[b_bs, P_["cst"]], writes=[b_wtab])
                                op("dve", (lambda g: g.tensor_tensor(out=bs[:, 1:2], in0=bs[:, 0:1], in1=wtab[:, 0:1],
                                                                     op=ALU.add)),
                                   reads=[b_bs, b_wtab], writes=[b_bs])

                            def emit_bisect_iter(tl, it):
                                qg_, nk, _ = tile_info(tl)
                                scb, b_scb = sc[tl % 2], b_sc[tl % 2]
                                scv = scb[:, 0:nk]
                                if it == 0:
                                    emit_bisect_init(tl)
                                if it % 2 == 0 or not ACT_COUNT:
                                    op("dve", (lambda g:
                                               g.tensor_scalar(out=junkb[:, 0:nk], in0=scv, scalar1=bs[:, 1:2],
                                                               scalar2=None, op0=ALU.is_ge, op1=ALU.add,
                                                               accum_out=bs[:, 2:3])),
                                       reads=[b_scb, b_bs], writes=[b_junkb, b_bs])
                                    op("dve", (lambda g:
                                               g.tensor_scalar(out=bs[:, 3:4], in0=bs[:, 2:3], scalar1=float(TOPK) - 0.5,
                                                               scalar2=wtab[:, it:it + 1], op0=ALU.is_ge, op1=ALU.mult)),
                                       reads=[b_bs, b_wtab], writes=[b_bs])
                                else:
                                    op("act", (lambda g:
                                               g.activation(out=junkb[:, 0:nk], in_=scv, func=AF.Sign, scale=-1.0,
                                                            bias=bs[:, 1:2], accum_out=bs[:, 6:7])),
                                       reads=[b_scb, b_bs], writes=[b_junkb, b_bs6])
                                    op("dve", (lambda g:
                                               g.tensor_scalar(out=bs[:, 3:4], in0=bs[:, 6:7],
                                                               scalar1=float(nk - 2 * TOPK) + 0.5,
                                                               scalar2=wtab[:, it:it + 1], op0=ALU.is_le, op1=ALU.mult)),
                                       reads=[b_bs6, b_wtab], writes=[b_bs])
                                nxt_w = it + 1 if it + 1 < NBIS else it
                                op("dve", (lambda g:
                                           g.scalar_tensor_tensor(out=bs[:, 1:2], in0=bs[:, 3:4],
                                                                  scalar=nwtab[:, nxt_w:nxt_w + 1], in1=bs[:, 1:2],
                                                                  op0=ALU.add, op1=ALU.add)),
                                   reads=[b_bs, b_wtab], writes=[b_bs])

                            def emit_masks(tl):
                                qg_, nk, _ = tile_info(tl)
                                si = tl % 2
                                scv = sc[tl % 2][:, 0:nk]
                                op("dve", (lambda g, scv=scv, nk=nk:
                                           g.tensor_scalar(out=mTS[:, 0:nk], in0=scv, scalar1=bs[:, 1:2],
                                                           scalar2=None, op0=ALU.is_ge)),
                                   reads=[b_sc[tl % 2], b_bs], writes=[b_mTS])
                                if "mTS" in dbg_d and seq == 0:
                                    op("dve", (lambda g, nk=nk: g.tensor_copy(out=mdbg[:, 0:nk], in_=mTS[:, 0:nk])),
                                       reads=[b_mTS], writes=[b_mdbg])
                                    dbg_dump("mTS", mdbg[:, 0:nk], [b_mdbg], row0=qg_ * 128)
                                for j0 in range(0, qg_ + 1, 8):
                                    nj = min(8, qg_ + 1 - j0)
                                    pb, pt = ps()
                                    ptb = pt.bitcast(BF16)
                                    for jj in range(nj):
                                        j = j0 + jj
                                        op("pe", (lambda g, ptb=ptb, jj=jj, j=j:
                                                  g.transpose(ptb[:, jj * 128:(jj + 1) * 128],
                                                              mTS[:, j * 128:(j + 1) * 128], identb[:])),
                                           reads=[b_mTS, P_["identb"]], writes=[pb])
                                    op("act", (lambda g, ptb=ptb, j0=j0, nj=nj, si=si:
                                               g.activation(out=mT[si][:, j0:j0 + nj, :].rearrange("p j t -> p (j t)"),
                                                            in_=ptb[:, 0:nj * 128], func=AF.Copy)),
                                       reads=[pb], writes=[b_mT[si]])

                            def att_steps(tl):
                                qg_, nk, select = tile_info(tl)
                                si = tl % 2
                                pairs = [(kvg, j) for kvg in range(2) for j in range(qg_ + 1)]
                                st = {}
                                LA = 2

                                def front(i):
                                    kvg, j = pairs[i]
                                    qrhs = qT[:, tl, kvg * 4:(kvg + 1) * 4, :].rearrange("p h t -> p (h t)")
                                    qb_reads = [b_qT[tl][kvg * 4 + hh] for hh in range(4)]
                                    pbs_, pts_ = ps()
                                    op("pe", (lambda g: g.matmul(pts_[:, 0:512], kT[:, kvg, j * 128:(j + 1) * 128], qrhs,
                                                                 start=True, stop=True)),
                                       reads=[P_["kT"]] + qb_reads, writes=[pbs_])
                                    p_i = ctr["pi"] % 4
                                    ctr["pi"] += 1
                                    op("act", (lambda g: g.activation(out=pT[p_i][:], in_=pts_[:, 0:512], func=AF.Exp,
                                                                      scale=att_scale)),
                                       reads=[pbs_], writes=[b_pT[p_i]])
                                    st[i] = p_i

                                def back(i):
                                    kvg, j = pairs[i]
                                    p_i = st.pop(i)
                                    qrhs = qT[:, tl, kvg * 4:(kvg + 1) * 4, :].rearrange("p h t -> p (h t)")
                                    qb_reads = [b_qT[tl][kvg * 4 + hh] for hh in range(4)]
                                    if j == 0:
                                        st["o%d" % kvg] = ps(hold=True)
                                        st["r%d" % kvg] = ps(hold=True)
                                    pbo, pto = st["o%d" % kvg]
                                    pbr, ptr = st["r%d" % kvg]
                                    if select or j == qg_:
                                        if select:
                                            msk, mreads = mT[si][:, j, :], [b_mT[si]]
                                        else:
                                            msk, mreads = trib[:], [P_["trib"]]
                                        op(MASK_ENG, (lambda g:
                                                   g.tensor_tensor(out=pT[p_i][:].rearrange("p (h t) -> p h t", h=4),
                                                                   in0=pT[p_i][:].rearrange("p (h t) -> p h t", h=4),
                                                                   in1=msk.unsqueeze(1).to_broadcast([128, 4, 128]),
                                                                   op=ALU.mult)),
                                           reads=[b_pT[p_i]] + mreads, writes=[b_pT[p_i]])
                                    op("pe", (lambda g: g.matmul(pto[:, 0:512], Vc[:, j, kvg, :], pT[p_i][:],
                                                                 start=(j == 0), stop=(j == qg_))),
                                       reads=[P_["Vc"], b_pT[p_i]], writes=[pbo])
                                    op("pe", (lambda g: g.matmul(ptr[:, 0:512], onesb[:], pT[p_i][:],
                                                                 start=(j == 0), stop=(j == qg_))),
                                       reads=[P_["onesb"], b_pT[p_i]], writes=[pbr])
                                    if j == qg_:
                                        op("dve", (lambda g: g.reciprocal(out=rinv[:], in_=ptr[:, 0:512])),
                                           reads=[pbr], writes=[b_rinv])
                                        op("dve", (lambda g: g.tensor_tensor(out=qrhs, in0=pto[:, 0:512], in1=rinv[:],
                                                                             op=ALU.mult)),
                                           reads=[pbo, b_rinv], writes=qb_reads)
                                        ps_release(pbo)
                                        ps_release(pbr)

                                steps = []
                                n = len(pairs)

                                def mk(i):
                                    def f():
                                        if i == 0:
                                            for k in range(min(LA, n)):
                                                front(k)
                                        if i + LA < n:
                                            front(i + LA)
                                        back(i)
                                    return f
                                for i in range(n):
                                    steps.append(mk(i))
                                return steps

                            def run_merged(*lists):
                                lists = [l for l in lists if l]
                                pos = [0] * len(lists)
                                while True:
                                    best, bi = None, None
                                    for i, l in enumerate(lists):
                                        if pos[i] < len(l):
                                            frac = (pos[i] + 0.5) / len(l)
                                            if best is None or frac < best:
                                                best, bi = frac, i
                                    if bi is None:
                                        break
                                    lists[bi][pos[bi]]()
                                    pos[bi] += 1

                            def sel(tl):
                                return tl < NT and tile_info(tl)[2]

                            def bis_steps(tl):
                                return [(lambda it=it: emit_bisect_iter(tl, it)) for it in range(NBIS)]

                            if sel(0):
                                run_merged(score_steps(0))
                                run_merged(bis_steps(0), score_steps(1) if sel(1) else [])
                                emit_masks(0)
                            elif sel(1):
                                run_merged(score_steps(1))
                            for tl in range(NT):
                                run_merged(bis_steps(tl + 1) if sel(tl + 1) else [], att_steps(tl),
                                           score_steps(tl + 2) if sel(tl + 2) else [])
                                if sel(tl + 1):
                                    emit_masks(tl + 1)
                                emit_mod_pending((16 + NT - 1) // NT if tl < NT - 1 else 16)

                            sch.barrier(skip=ring_ds + ring_st)
                        if "yb" in dbg_d and seq == 0:
                            for tt in range(NT):
                                dbg_dump("yb", qT[:, tt, :, :].rearrange("p h t -> p (h t)"), b_qT[tt],
                                         col0=(m * NT + tt) * 1024)

                        p3_es = ExitStack()
                        with p3_es:
                            sg = [sb("sg%d" % i, [128, SUBW], F32, p3_es) for i in range(2)]
                            b_sg = [Buf("sg%d" % i) for i in range(2)]
                            tmpm = [sb("tmpm%d" % i, [128, SUBW], BF16, p3_es) for i in range(2)]
                            b_tmpm = [Buf("tmpm%d" % i) for i in range(2)]
                            gi = [0]

                            def gated_proj(w_g_col, w_p_d, KCP, act_of, act_reads, accumulate):
                                for nb in range(8):
                                    rbg, rtg = ring_load([(0, KC * 256, wsrc(w_in_d, 0, KC, w_g_col + nb * 256, 256)),
                                                          (KC * 256, KCP * 256, wsrc(w_p_d, 0, KCP, nb * 256, 256))])
                                    wvg = rtg[:, 0:KC * 256].rearrange("p (k n) -> p k n", k=KC)
                                    rbp = rbg
                                    wvp = rtg[:, KC * 256:(KC + KCP) * 256].rearrange("p (k n) -> p k n", k=KCP)
                                    for cj in range(2):
                                        f = nb * 2 + cj
                                        for sub in range(NSUB):
                                            i = gi[0] % 2
                                            gi[0] += 1
                                            pbg, ptg = ps()
                                            for kc in range(KC):
                                                op("pe", (lambda g, ptg=ptg, kc=kc, cj=cj, sub=sub, wvg=wvg:
                                                          g.matmul(ptg[:, 0:SUBW], wvg[:, kc, cj * 128:(cj + 1) * 128],
                                                                   h_of(kc, sub), start=(kc == 0), stop=(kc == KC - 1))),
                                                   reads=[rbg, hT_b[sub][kc]], writes=[pbg])
                                            op("act", (lambda g, ptg=ptg, i=i:
                                                       g.activation(out=sg[i][:], in_=ptg[:, 0:SUBW], func=AF.Sigmoid)),
                                               reads=[pbg], writes=[b_sg[i]])
                                            pbp, ptp = ps()
                                            for kc in range(KCP):
                                                op("pe", (lambda g, ptp=ptp, kc=kc, cj=cj, sub=sub, wvp=wvp:
                                                          g.matmul(ptp[:, 0:SUBW], wvp[:, kc, cj * 128:(cj + 1) * 128],
                                                                   act_of(kc, sub), start=(kc == 0),
                                                                   stop=(kc == KCP - 1))),
                                                   reads=[rbp] + act_reads(kc, sub), writes=[pbp])
                                            dst = mg[:, f, sub * SUBW:(sub + 1) * SUBW]
                                            if not accumulate:
                                                op("dve", (lambda g, dst=dst, ptp=ptp, i=i:
                                                           g.tensor_tensor(out=dst, in0=ptp[:, 0:SUBW], in1=sg[i][:],
                                                                           op=ALU.mult)),
                                                   reads=[pbp, b_sg[i]], writes=[b_mg[f][sub]])
                                            else:
                                                op("dve", (lambda g, ptp=ptp, i=i:
                                                           g.tensor_tensor(out=tmpm[i][:], in0=ptp[:, 0:SUBW],
                                                                           in1=sg[i][:], op=ALU.mult)),
                                                   reads=[pbp, b_sg[i]], writes=[b_tmpm[i]])
                                                op("dve", (lambda g, dst=dst, i=i:
                                                           g.tensor_tensor(out=dst, in0=dst, in1=tmpm[i][:], op=ALU.add)),
                                                   reads=[b_tmpm[i], b_mg[f][sub]], writes=[b_mg[f][sub]])

                            def yb_of(kc, sub):
                                return qT[:, sub * TPS:(sub + 1) * TPS, kc, :]

                            def yb_reads(kc, sub):
                                return [b_qT[sub * TPS + t_][kc] for t_ in range(TPS)]
                            gated_proj(C_GB, w_pb_d, 8, yb_of, yb_reads, False)
                            sch.barrier(skip=ring_ds + ring_st)
                    if "mB" in dbg_d and seq == 0:
                        for f in range(KC):
                            for sub in range(NSUB):
                                dbg_dump("mB", mg[:, f, sub * SUBW:(sub + 1) * SUBW], [b_mg[f][sub]], row0=f * 128,
                                         col0=m * MT + sub * SUBW)

                    a_es = ExitStack()
                    with a_es:
                        uT = sb("uT", [128, 8, MT], BF16, a_es)
                        b_uT = [[Buf("uT%d_%d" % (c_, t_)) for t_ in range(NT)] for c_ in range(8)]
                        vv = sb("vv", [128, NT, A_W], BF16, a_es)
                        b_vv = [Buf("vv%d" % t_) for t_ in range(NT)]
                        vss = sb("vss", [128, NT], F32, a_es)
                        b_vss = Buf("vss")
                        vjunk = sb("vjunk", [128, A_W], BF16, a_es)
                        b_vjunk = Buf("vjunk")
                        tmpa = [sb("tmpa%d" % i, [128, 4, 128], F32, a_es) for i in range(2)]
                        b_tmpa = [Buf("tmpa%d" % i) for i in range(2)]
                        for ug in range(2):
                            rb, rt = ring_load([(0, KC * 512, wsrc(w_in_d, 0, KC, C_U + ug * 512, 512))])
                            wv = rt[:, 0:KC * 512].rearrange("p (k n) -> p k n", k=KC)

                            def ev_u(cj, sub, pb, pt, ug=ug):
                                c = ug * 4 + cj
                                op("act", (lambda g: g.activation(out=uT[:, c, sub * SUBW:(sub + 1) * SUBW],
                                                                  in_=pt[:, 0:SUBW], func=AF.Gelu_apprx_tanh)),
                                   reads=[pb], writes=[b_uT[c][sub * TPS + t_] for t_ in range(TPS)])
                            gemm_F(rb, wv, 4, KC, h_of, h_reads, ev_u)
                        for vg in range(2):
                            rb, rt = ring_load([(0, KC * 512, wsrc(w_in_d, 0, KC, C_VA + vg * 512, 512))])
                            wv = rt[:, 0:KC * 512].rearrange("p (k n) -> p k n", k=KC)
                            for tt in range(NT):
                                pb, pt = ps()
                                sub, ti = divmod(tt, TPS)
                                for kc in range(KC):
                                    op("pe", (lambda g, pt=pt, kc=kc, sub=sub, ti=ti, wv=wv:
                                              g.matmul(pt[:, 0:512], hT[:, kc, 2 + sub * SUBW + ti * 128:
                                                                        2 + sub * SUBW + (ti + 1) * 128],
                                                       wv[:, kc, :], start=(kc == 0), stop=(kc == KC - 1))),
                                       reads=[rb, hT_b[sub][kc]], writes=[pb])
                                op("act", (lambda g, pt=pt, tt=tt, vg=vg:
                                           g.activation(out=vv[:, tt, vg * 512:(vg + 1) * 512], in_=pt[:, 0:512],
                                                        func=AF.Gelu_apprx_tanh)),
                                   reads=[pb], writes=[b_vv[tt]])
                        for tt in range(NT):
                            op("act", (lambda g, tt=tt: g.activation(out=vjunk[:], in_=vv[:, tt, :], func=AF.Square,
                                                                     accum_out=vss[:, tt:tt + 1])),
                               reads=[b_vv[tt]], writes=[b_vjunk, b_vss])
                        op("act", lambda g: g.activation(out=vss[:], in_=vss[:], func=AF.Sqrt, scale=1.0 / A_W,
                                                         bias=epsT[:, 0:1]),
                           reads=[b_vss, P_["epsT"]], writes=[b_vss])
                        op("dve", lambda g: g.reciprocal(out=vss[:], in_=vss[:]), reads=[b_vss], writes=[b_vss])
                        for tt in range(NT):
                            op("dve", (lambda g, tt=tt: g.tensor_scalar(out=vv[:, tt, :], in0=vv[:, tt, :],
                                                                        scalar1=vss[:, tt:tt + 1], scalar2=None,
                                                                        op0=ALU.mult)),
                               reads=[b_vv[tt], b_vss], writes=[b_vv[tt]])
                        ai = 0
                        for tt in range(NT):
                            for g4 in range(2):
                                pb, pt = ps()
                                for gg in range(4):
                                    gi_ = g4 * 4 + gg
                                    op("pe", (lambda g, pt=pt, gg=gg, gi_=gi_, tt=tt:
                                              g.matmul(pt[:, gg * 128:(gg + 1) * 128], vv[:, tt, gi_ * 128:(gi_ + 1) * 128],
                                                       wspT[:, gi_, :], start=True, stop=True)),
                                       reads=[b_vv[tt], P_["wspT"]], writes=[pb])
                                a_i = ai % 2
                                ai += 1
                                for gg in range(4):
                                    gi_ = g4 * 4 + gg
                                    op("dve", (lambda g, pt=pt, gg=gg, gi_=gi_, a_i=a_i:
                                               g.scalar_tensor_tensor(out=tmpa[a_i][:, gg, :],
                                                                      in0=pt[:, gg * 128:(gg + 1) * 128],
                                                                      scalar=pv("v_norm_g", gi_, gi_ + 1),
                                                                      in1=pv("bsp_rep", gi_ * 128, (gi_ + 1) * 128),
                                                                      op0=ALU.mult, op1=ALU.add)),
                                       reads=[pb, P_["prm"]], writes=[b_tmpa[a_i]])
                                uview = uT[:, g4 * 4:(g4 + 1) * 4, tt * 128:(tt + 1) * 128]
                                op("dve", (lambda g, uview=uview, a_i=a_i:
                                           g.tensor_tensor(out=uview, in0=uview, in1=tmpa[a_i][:], op=ALU.mult)),
                                   reads=[b_tmpa[a_i]] + [b_uT[g4 * 4 + gg][tt] for gg in range(4)],
                                   writes=[b_uT[g4 * 4 + gg][tt] for gg in range(4)])
                        if "ya" in dbg_d and seq == 0:
                            for c in range(8):
                                dbg_dump("ya", uT[:, c, :], b_uT[c], row0=c * 128, col0=m * MT)
                        p5_es = ExitStack()
                        with p5_es:
                            sg = [sb("sg5_%d" % i, [128, SUBW], F32, p5_es) for i in range(2)]
                            b_sg = [Buf("sg%d" % i) for i in range(2)]
                            tmpm = [sb("tmpm5_%d" % i, [128, SUBW], BF16, p5_es) for i in range(2)]
                            b_tmpm = [Buf("tmpm%d" % i) for i in range(2)]

                            def ya_of(kc, sub):
                                return uT[:, kc, sub * SUBW:(sub + 1) * SUBW]

                            def ya_reads(kc, sub):
                                return [b_uT[kc][sub * TPS + t_] for t_ in range(TPS)]
                            gated_proj(C_GA, w_pa_d, 8, ya_of, ya_reads, True)
                            sch.barrier(skip=ring_ds + ring_st)
                    if "merged" in dbg_d and seq == 0:
                        for f in range(KC):
                            for sub in range(NSUB):
                                dbg_dump("merged", mg[:, f, sub * SUBW:(sub + 1) * SUBW], [b_mg[f][sub]], row0=f * 128,
                                         col0=m * MT + sub * SUBW)

                    p6_es = ExitStack()
                    with p6_es:
                        def mg_of(kc, sub):
                            return mg[:, kc, sub * SUBW:(sub + 1) * SUBW]

                        def mg_reads(kc, sub):
                            return [b_mg[kc][sub]]
                        h2_b = [[Buf("h2T%d_%d" % (s_, k)) for k in range(KC)] for s_ in range(NSUB)]

                        ntmp6 = norm_tmp(p6_es)

                        def after6(sub, xo, b_xo):
                            norm_to_hT(ntmp6, xo, b_xo, 1, seq, hT, h2_b, sub, 2)
                        op("dve", lambda g: g.tensor_copy(out=hT[:, :, 0:2], in_=halo[:]),
                           reads=[P_["halo"]], writes=[hT_halo_b])
                        down_proj(p6_es, w_out_d, KC, mg_of, mg_reads, G1, seq,
                                  lambda sub: x_d[tok0 + sub * SUBW: tok0 + (sub + 1) * SUBW, :],
                                  lambda sub: out_d[tok0 + sub * SUBW: tok0 + (sub + 1) * SUBW, :],
                                  lambda sub: dram_x_b, lambda sub: dob(seq, m, sub), after_sub=after6, tag="6")
                        op("dve", lambda g: g.tensor_copy(out=halo[:], in_=hT[:, :, MT:MT + 2]),
                           reads=[h2_b[NSUB - 1][k] for k in range(KC)], writes=[P_["halo"]])
                        sch.barrier(skip=ring_ds + ring_st)
                    mg_es.close()
                    if m == NMT - 1:
                        op("dve", lambda g: g.memset(halo[:], 0.0), writes=[P_["halo"]])
                    if "h2T" in dbg_d and seq == 0:
                        for kc in range(KC):
                            for sub in range(NSUB):
                                dbg_dump("h2T", h_of(kc, sub), [h2_b[sub][kc]], row0=kc * 128,
                                         col0=m * MT + sub * SUBW)

                    f_es = ExitStack()
                    with f_es:
                        gT = sb("gT", [128, FC, MT], BF16, f_es)
                        b_gT = [[Buf("gT%d_%d" % (c_, s_)) for s_ in range(NSUB)] for c_ in range(FC)]
                        p7_es = ExitStack()
                        with p7_es:
                            ya_ = [sb("cy%d" % i, [128, 512], F32, p7_es) for i in range(4)]
                            b_ya = [Buf("cy%d" % i) for i in range(4)]
                            b_yab = [Buf("cyb%d" % i) for i in range(4)]
                            yi = 0
                            W = SUBW
                            for c2 in range(FC // 2):
                                rb, rt = ring_load([(0, KC * 256, wsrc(w_up_d, 0, KC, c2 * 256, 256)),
                                                    (KC * 256, KC * 256, wsrc(w_up_d, 0, KC, DFF + c2 * 256, 256))])
                                wva = rt[:, 0:KC * 256].rearrange("p (k n) -> p k n", k=KC)
                                wvb = rt[:, KC * 256:KC * 512].rearrange("p (k n) -> p k n", k=KC)
                                for cc in range(2):
                                    c = c2 * 2 + cc
                                    for sub in range(NSUB):
                                        res = []
                                        for ab, wv_ in ((0, wva), (1, wvb)):
                                            ch = c + ab * FC
                                            pb, pt = ps()
                                            for kc in range(KC):
                                                op("pe", (lambda g, pt=pt, wv_=wv_, kc=kc, cc=cc, sub=sub:
                                                          g.matmul(pt[:, 0:W], wv_[:, kc, cc * 128:(cc + 1) * 128],
                                                                   h_of(kc, sub), start=(kc == 0), stop=(kc == KC - 1))),
                                                   reads=[rb, h2_b[sub][kc]], writes=[pb])
                                            y_i = yi % 4
                                            yi += 1
                                            yt, b_y, b_yb = ya_[y_i], b_ya[y_i], b_yab[y_i]
                                            cw = lambda j, ch=ch: pv("conv_w", j * 88 + ch, j * 88 + ch + 1)
                                            cb = pv("conv_b", ch, ch + 1)
                                            ux = uhx[:, ch, :]
                                            op("act", (lambda g, pt=pt, yt=yt, cw=cw, cb=cb:
                                                       g.activation(out=yt[:, 2:W], in_=pt[:, 2:W], func=AF.Identity,
                                                                    scale=cw(2), bias=cb)),
                                               reads=[pb, P_["prm"]], writes=[b_y])
                                            op("dve", (lambda g, pt=pt, yt=yt, cw=cw:
                                                       g.scalar_tensor_tensor(out=yt[:, 2:W], in0=pt[:, 1:W - 1], scalar=cw(1),
                                                                              in1=yt[:, 2:W], op0=ALU.mult, op1=ALU.add)),
                                               reads=[pb, b_y, P_["prm"]], writes=[b_y])
                                            op("dve", (lambda g, pt=pt, yt=yt, cw=cw:
                                                       g.scalar_tensor_tensor(out=yt[:, 2:W], in0=pt[:, 0:W - 2], scalar=cw(0),
                                                                              in1=yt[:, 2:W], op0=ALU.mult, op1=ALU.add)),
                                               reads=[pb, b_y, P_["prm"]], writes=[b_y])
                                            op("dve", (lambda g, pt=pt, ux=ux: g.tensor_copy(out=ux[:, 2:4], in_=pt[:, 0:2])),
                                               reads=[pb], writes=[b_uh[ch]])
                                            op("act", (lambda g, yt=yt, ux=ux, cw=cw, cb=cb:
                                                       g.activation(out=yt[:, 0:2], in_=ux[:, 2:4], func=AF.Identity,
                                                                    scale=cw(2), bias=cb)),
                                               reads=[b_uh[ch], P_["prm"]], writes=[b_yb])
                                            op("dve", (lambda g, yt=yt, ux=ux, cw=cw:
                                                       g.scalar_tensor_tensor(out=yt[:, 0:2], in0=ux[:, 1:3], scalar=cw(1),
                                                                              in1=yt[:, 0:2], op0=ALU.mult, op1=ALU.add)),
                                               reads=[b_uh[ch], b_yb, P_["prm"]], writes=[b_yb])
                                            op("dve", (lambda g, yt=yt, ux=ux, cw=cw:
                                                       g.scalar_tensor_tensor(out=yt[:, 0:2], in0=ux[:, 0:2], scalar=cw(0),
                                                                              in1=yt[:, 0:2], op0=ALU.mult, op1=ALU.add)),
                                               reads=[b_uh[ch], b_yb, P_["prm"]], writes=[b_yb])
                                            op("dve", (lambda g, pt=pt, ux=ux: g.tensor_copy(out=ux[:, 0:2], in_=pt[:, W - 2:W])),
                                               reads=[pb], writes=[b_uh[ch]])
                                            res.append((yt, b_y, b_yb))
                                        (yta, b_a, b_ab), (ytb, b_b, b_bb) = res
                                        op("act", (lambda g, yta=yta:
                                                   g.activation(out=yta[:, 0:W], in_=yta[:, 0:W], func=AF.Silu)),
                                           reads=[b_a, b_ab], writes=[b_a, b_ab])
                                        op("dve", (lambda g, yta=yta, ytb=ytb, c=c, sub=sub:
                                                   g.tensor_tensor(out=gT[:, c, sub * W:(sub + 1) * W], in0=yta[:, 0:W],
                                                                   in1=ytb[:, 0:W], op=ALU.mult)),
                                           reads=[b_a, b_ab, b_b, b_bb], writes=[b_gT[c][sub]])
                            sch.barrier(skip=ring_ds + ring_st)
                        if m == NMT - 1:
                            op("dve", lambda g: g.memset(uhx[:], 0.0), writes=b_uh)
                        if "gT" in dbg_d and seq == 0:
                            for c in range(FC):
                                dbg_dump("gT", gT[:, c, :], b_gT[c], row0=c * 128, col0=m * MT)
                        p8_es = ExitStack()
                        with p8_es:
                            def g_of(kc, sub):
                                return gT[:, kc, sub * SUBW:(sub + 1) * SUBW]

                            def g_reads(kc, sub):
                                return [b_gT[kc][sub]]
                            down_proj(p8_es, w_dn_d, FC, g_of, g_reads, G2, seq,
                                      lambda sub: out_d[tok0 + sub * SUBW: tok0 + (sub + 1) * SUBW, :],
                                      lambda sub: out_d[tok0 + sub * SUBW: tok0 + (sub + 1) * SUBW, :],
                                      lambda sub: dob(seq, m, sub), lambda sub: dob(seq, m, sub), tag="8", xo=xo8)
                            sch.barrier(skip=ring_ds + ring_st)
        fin_reads = list(dram_out_b.values())
        op("sp", lambda g: g.nop(), reads=fin_reads + [Buf("dummy")], writes=[])
        sch.barrier()
        op("sp", lambda g: g.nop(), reads=[Buf("dummy2")], writes=[])
        sch.flush()
    return nc


def CST_N(NBIS):
    return 512 + NBIS


def PRM_OFF(NSEQ):
    names = (("cT", KC * NSEQ), ("b_ada", 96), ("norm1_g", 16), ("norm2_g", 16), ("v_norm_g", 8), ("q_norm_g", 1),
             ("k_norm_g", 1), ("conv_w", 3 * 88), ("conv_b", 88), ("bsp_rep", 8 * 128))
    off = {}
    o = 0
    for n, k in names:
        off[n] = (o, k)
        o += k
    off["_total"] = (o, 0)
    return off


def PRM_N(NSEQ):
    return PRM_OFF(NSEQ)["_total"][0]


def _pp(v):
    v = np.asarray(v, np.float32)
    return np.ascontiguousarray(v.reshape(-1, 128).T)


def make_consts(NBIS):
    c = np.zeros((128, CST_N(NBIS)), np.float32)
    i = np.arange(128)
    c[:, 0:128] = np.eye(128, dtype=np.float32)
    c[:, 128:256] = (i[:, None] <= i[None, :]).astype(np.float32)
    c[:, 256:384] = np.where(i[None, :] <= i[:, None], 0.0, -1e30)
    c[:, 384:512] = 1.0
    c[:, 512:512 + NBIS] = (0.5 ** (np.arange(NBIS) + 1))[None, :]
    return c


def make_params(c_rows, b_ada, norm1_g, norm2_g, v_norm_g, q_norm_g, k_norm_g, conv_w, conv_b, b_spatial):
    NSEQ = c_rows.shape[0]
    off = PRM_OFF(NSEQ)
    p = np.zeros((128, PRM_N(NSEQ)), np.float32)

    def put(name, arr):
        o, n = off[name]
        p[:, o:o + n] = np.asarray(arr, np.float32).reshape(128, n)
    cT = np.stack([_pp(c_rows[b]) for b in range(NSEQ)], axis=-1)
    put("cT", cT)
    put("b_ada", _pp(b_ada))
    put("norm1_g", _pp(norm1_g))
    put("norm2_g", _pp(norm2_g))
    put("v_norm_g", _pp(v_norm_g))
    put("q_norm_g", np.asarray(q_norm_g).reshape(128, 1))
    put("k_norm_g", np.asarray(k_norm_g).reshape(128, 1))
    cw = np.stack([_pp(conv_w[j]) for j in range(3)], axis=1)
    put("conv_w", cw)
    put("conv_b", _pp(conv_b))
    put("bsp_rep", np.broadcast_to(np.asarray(b_spatial, np.float32).reshape(1, 8 * 128), (128, 8 * 128)))
    return p


def make_in_maps(inputs, n_cores, NSEQ, NBIS, S):
    f = lambda a: np.ascontiguousarray(np.asarray(a, dtype=np.float32))
    x = f(inputs["x"])
    c = f(inputs["c"])
    cst = make_consts(NBIS)
    wspT = np.ascontiguousarray(f(inputs["w_spatial"]).transpose(2, 0, 1).reshape(128, 8 * 128))
    shared = {k: f(inputs[k]) for k in ("w_ada", "w_in", "w_proj_a", "w_proj_b", "w_out", "w_up", "w_down")}
    maps = []
    for i in range(n_cores):
        b0 = i * NSEQ
        m = dict(shared)
        m["x"] = np.ascontiguousarray(x[b0:b0 + NSEQ].reshape(NSEQ * S, D))
        m["cst"] = cst
        m["prm"] = make_params(c[b0:b0 + NSEQ], inputs["b_ada"], inputs["norm1_g"], inputs["norm2_g"],
                               inputs["v_norm_g"], inputs["q_norm_g"], inputs["k_norm_g"], inputs["conv_w"],
                               inputs["conv_b"], inputs["b_spatial"])
        m["wspT"] = wspT
        maps.append(m)
    return maps


def kernel(**inputs):
    S, NSEQ, NBIS = 2048, 2, 24
    nc = build_nc(S=S, NSEQ=NSEQ, MT=1024, NBIS=NBIS)
    maps = make_in_maps(inputs, N_CORES, NSEQ, NBIS, S)
    res = run_bass_kernel_spmd(nc, maps, core_ids=list(range(N_CORES)))
    outs = [np.asarray(r["out"], dtype=np.float32).reshape(NSEQ, S, D) for r in res.results]
    return np.concatenate(outs, axis=0)
```

```python
import numpy as np
from contextlib import ExitStack
import concourse.bass as bass
import concourse.mybir as mybir
from concourse.bass_utils import run_bass_kernel_spmd

F32 = mybir.dt.float32
BF16 = mybir.dt.bfloat16
AF = mybir.ActivationFunctionType
ALU = mybir.AluOpType
AX = mybir.AxisListType

D = 2048
KC = D // 128
A_W = 1024
QW = 1024
KVW = 256
NIH = 16
IDD = 64
DFF = 5632
FC = DFF // 128
EPS = 1e-6
N_CORES = 8
ENGS = ("pe", "act", "dve", "pool", "sp")
MASK_ENG = "pool"
ACT_COUNT = True

C_U, C_VA, C_Q, C_K, C_VB, C_QI, C_KI, C_WI, C_GA, C_GB = 0, 1024, 2048, 3072, 3328, 3584, 4608, 4672, 4688, 6736
NCOLS_IN = 8784


class DSem:
    def __init__(self, h):
        self.h = h
        self.cnt = 0


class Buf:
    __slots__ = ("name", "w", "rs", "local")

    def __init__(self, name, local=True):
        self.name = name
        self.w = None
        self.rs = {}
        self.local = local


class Sched:
    def __init__(self, nc, es):
        self.nc = nc
        self.eng = {"pe": nc.tensor, "act": nc.scalar, "dve": nc.vector, "pool": nc.gpsimd, "sp": nc.sync}
        self.pending = []
        self.ecount = {e: 0 for e in ENGS}
        self.last_compute = {e: None for e in ENGS}
        self.signal = {e: set() for e in ENGS}
        self.rank = {e: {} for e in ENGS}
        self.sigc = {e: 0 for e in ENGS}
        self.flushed = {e: -1 for e in ENGS}
        self.flushed_sig = {e: None for e in ENGS}
        self.waited = {e: {} for e in ENGS}
        self.bar = {e: None for e in ENGS}
        self.sems = {e: es.enter_context(nc.semaphore("s_" + e)) for e in ENGS}
        self.es = es
        self.dsems = []
        self.n_wait = 0

    def dsem(self, name):
        d = DSem(self.es.enter_context(self.nc.semaphore("d_" + name)))
        self.dsems.append(d)
        return d

    def op(self, e, fn, reads=(), writes=(), dsem=None):
        need = {}

        def add(ev, kind):
            v = ev[2]
            if ev[0] == "e":
                if ev[1] == e and (e == "pe" or kind == "bar"):
                    return
                if v <= self.flushed[ev[1]] and v not in self.rank[ev[1]]:
                    v = self.flushed_sig[ev[1]]
            k = (ev[0], ev[1])
            if v > need.get(k, -1):
                need[k] = v

        touches_local = False
        for b in reads:
            touches_local |= b.local
            if b.w is not None:
                add(b.w, "raw")
        for b in writes:
            touches_local |= b.local
            if b.w is not None:
                add(b.w, "waw")
            for k, v in b.rs.items():
                add((k[0], k[1], v), "war")
        if self.bar[e] is not None and touches_local:
            for ev in self.bar[e]:
                add(ev, "bar")
            self.bar[e] = None
        idx = self.ecount[e]
        self.ecount[e] += 1
        final = []
        wd = self.waited[e]
        for k, v in need.items():
            if v <= wd.get(k, -1):
                continue
            wd[k] = v
            final.append((k[0], k[1], v))
            if k[0] == "e":
                self.signal[k[1]].add(v)
        if dsem is not None:
            dsem.cnt += 1
            me = ("d", dsem, 16 * dsem.cnt)
        else:
            me = ("e", e, idx)
            self.last_compute[e] = idx
        mk = (me[0], me[1])
        for b in reads:
            if b.rs.get(mk, -1) < me[2]:
                b.rs[mk] = me[2]
        for b in writes:
            b.w = me
            b.rs = {}
        self.pending.append((e, fn, final, dsem, idx))

    def barrier(self, skip=()):
        evs = []
        for e in ENGS:
            if self.last_compute[e] is not None:
                evs.append(("e", e, self.last_compute[e]))
                self.signal[e].add(self.last_compute[e])
        for d in self.dsems:
            if d.cnt > 0 and d not in skip:
                evs.append(("d", d, 16 * d.cnt))
        for e in ENGS:
            self.bar[e] = list(evs)
        self.flush()

    def flush(self):
        for e in ENGS:
            if self.last_compute[e] is not None:
                self.signal[e].add(self.last_compute[e])
        for (e, fn, deps, dsem, idx) in self.pending:
            g = self.eng[e]
            for ev in deps:
                self.n_wait += 1
                if ev[0] == "e":
                    g.wait_ge(self.sems[ev[1]], self.rank[ev[1]][ev[2]])
                else:
                    g.wait_ge(ev[1].h, ev[2])
            ins = fn(g)
            if dsem is not None:
                ins.then_inc(dsem.h, 16)
            elif idx in self.signal[e]:
                self.sigc[e] += 1
                self.rank[e][idx] = self.sigc[e]
                ins.then_inc(self.sems[e], 1)
        self.pending = []
        for e in ENGS:
            self.flushed[e] = self.ecount[e] - 1
            self.flushed_sig[e] = self.last_compute[e]


def build_nc(S=2048, NSEQ=2, MT=1024, NBIS=24, dbg=()):
    TOPK = min(256, S // 4)
    NMT = S // MT
    NT = MT // 128
    SUBW = min(512, MT)
    NSUB = MT // SUBW
    TPS = SUBW // 128
    NCH = S // 128
    nc = bass.Bass("TRN2", target_bir_lowering=False)
    dram = {}

    def din(name, shape):
        dram[name] = nc.dram_tensor(name, list(shape), F32, kind="ExternalInput").ap()
        return dram[name]

    x_d = din("x", [NSEQ * S, D])
    cst_d = din("cst", [128, CST_N(NBIS)])
    prm_d = din("prm", [128, PRM_N(NSEQ)])
    wspT_d = din("wspT", [128, 8 * 128])
    w_ada_d = din("w_ada", [D, 6 * D])
    w_in_d = din("w_in", [D, NCOLS_IN])
    w_pa_d = din("w_proj_a", [A_W, D])
    w_pb_d = din("w_proj_b", [QW, D])
    w_out_d = din("w_out", [D, D])
    w_up_d = din("w_up", [D, 2 * DFF])
    w_dn_d = din("w_down", [DFF, D])
    out_d = nc.dram_tensor("out", [NSEQ * S, D], F32, kind="ExternalOutput").ap()
    wsc_d = nc.dram_tensor("wscratch", [72, 128, 8192], BF16, kind="Internal").ap()
    dbg_d = {}
    for name, shape in dbg:
        dbg_d[name] = nc.dram_tensor("dbg_" + name, list(shape), F32, kind="ExternalOutput").ap()

    es = ExitStack()
    with es:
        sch = Sched(nc, es)
        op = sch.op

        sb_n = [0]

        def sb(name, shape, dt, stack=es):
            sb_n[0] += 1
            return stack.enter_context(nc.sbuf_tensor("%s_%d" % (name, sb_n[0]), list(shape), dt))

        cst = sb("cst", [128, CST_N(NBIS)], F32)
        prm = sb("prm", [128, PRM_N(NSEQ)], F32)
        identb = sb("identb", [128, 128], BF16)
        trib = sb("trib", [128, 128], BF16)
        onesb = sb("onesb", [128, 128], BF16)
        wspT = sb("wspTb", [128, 8, 128], BF16)
        modT = sb("modT", [128, 96, NSEQ], F32)
        gm = sb("gm", [128, NSEQ, 2, KC], F32)
        scT = sb("scT", [128, KC, NSEQ], BF16)
        epsT = sb("epsT", [128, 1], F32)
        kT = sb("kT", [128, 2, S], BF16)
        Vc = sb("Vc", [128, NCH, 2, 128], BF16)
        kiT = sb("kiT", [128, 2, S], BF16)
        halo = sb("halo", [128, KC, 2], BF16)
        uhx = sb("uhx", [128, 2 * FC, 4], F32)
        b_uh = [Buf("uh%d" % i, local=False) for i in range(2 * FC)]
        RING_E = 8192
        NRING = 2
        ring = [sb("ring%d" % i, [128, RING_E], BF16) for i in range(NRING)]
        ring_b = [Buf("ring%d" % i, local=False) for i in range(NRING)]
        ring_ds = [sch.dsem("ring%d" % i) for i in range(NRING)]
        ring_st = [sch.dsem("ringst%d" % i) for i in range(NRING)]
        ring_i = [0]
        P_ = {k: Buf(k, local=False) for k in ("cst", "prm", "identb", "trib", "onesb", "wspT", "modT", "gm", "scT",
                                               "epsT", "kT", "Vc", "kiT", "halo")}
        ds_misc = sch.dsem("misc")
        ds_x4 = sch.dsem("x4")
        ds_xo = sch.dsem("xo")
        ds_dbg = sch.dsem("dbg")
        psf = [es.enter_context(nc.psum_tensor("ps%d" % i, [128, 512], F32)) for i in range(8)]
        ps_b = [Buf("ps%d" % i, local=False) for i in range(8)]
        ps_i = [0]

        ps_held = set()

        def ps(hold=False):
            while True:
                i = ps_i[0] % 8
                ps_i[0] += 1
                if i not in ps_held:
                    break
            if hold:
                ps_held.add(i)
            return ps_b[i], psf[i]

        def ps_release(pb):
            ps_held.discard(ps_b.index(pb))

        ident_f = cst[:, 0:128]
        tri_f = cst[:, 128:256]
        negbig = cst[:, 256:384]
        ones_f = cst[:, 384:512]
        pow2 = cst[:, 512:512 + NBIS]
        PO = PRM_OFF(NSEQ)

        def pv(name, a=None, b=None):
            o, n = PO[name]
            if a is None:
                return prm[:, o:o + n]
            return prm[:, o + a:o + b]

        op("sp", lambda g: g.dma_start(out=cst[:], in_=cst_d[:, :]), writes=[P_["cst"]], dsem=sch.dsem("cst"))
        op("sp", lambda g: g.dma_start(out=prm[:], in_=prm_d[:, :]), writes=[P_["prm"]], dsem=sch.dsem("prm"))
        op("pool", lambda g: g.dma_start(out=wspT[:].rearrange("p g t -> p (g t)"), in_=wspT_d[:, :]),
           writes=[P_["wspT"]], dsem=ds_misc)
        op("dve", lambda g: g.tensor_copy(out=identb[:], in_=ident_f), reads=[P_["cst"]], writes=[P_["identb"]])
        op("dve", lambda g: g.tensor_copy(out=trib[:], in_=tri_f), reads=[P_["cst"]], writes=[P_["trib"]])
        op("dve", lambda g: g.tensor_copy(out=onesb[:], in_=ones_f), reads=[P_["cst"]], writes=[P_["onesb"]])
        op("dve", lambda g: g.memset(epsT[:], EPS), writes=[P_["epsT"]])
        op("dve", lambda g: g.memset(halo[:], 0.0), writes=[P_["halo"]])
        op("dve", lambda g: g.memset(kiT[:], 0.0), writes=[P_["kiT"]])
        op("dve", lambda g: g.memset(uhx[:], 0.0), writes=b_uh)
        op("dve", lambda g: g.tensor_tensor(out=wspT[:], in0=wspT[:],
                                            in1=trib[:].unsqueeze(1).to_broadcast([128, 8, 128]), op=ALU.mult),
           reads=[P_["wspT"], P_["trib"]], writes=[P_["wspT"]])
        op("act", lambda g: g.activation(out=scT[:].rearrange("p k b -> p (k b)"), in_=pv("cT"), func=AF.Silu),
           reads=[P_["prm"]], writes=[P_["scT"]])

        wsc_keys = {}
        WSC_MAX = 72

        def ring_load(parts, cache=True, nused=None):
            i = ring_i[0] % NRING
            ring_i[0] += 1
            key = tuple((p[2].tensor.name, int(p[2].offset), tuple(tuple(a) for a in p[2].ap)) for p in parts)
            if nused is None:
                nused = max(p[0] + p[1] for p in parts)
            if cache and key in wsc_keys:
                gid, b_sc_ = wsc_keys[key]
                op("pool", (lambda g: g.dma_start(out=ring[i][:, 0:nused], in_=wsc_d[gid, :, 0:nused])),
                   reads=[b_sc_], writes=[ring_b[i]], dsem=ring_ds[i])
                return ring_b[i], ring[i]
            for part in parts:
                off, n, src = part[0], part[1], part[2]
                if len(part) > 3:
                    dst = part[3](ring[i])
                else:
                    dst = ring[i][:, off:off + n]
                    if len(src.shape) == 3:
                        dst = dst.rearrange("p (k n) -> p k n", k=src.shape[1])
                op("pool", (lambda g, dst=dst, src=src: g.dma_start(out=dst, in_=src)),
                   writes=[ring_b[i]], dsem=ring_ds[i])
            if cache and len(wsc_keys) < WSC_MAX:
                gid = len(wsc_keys)
                b_sc_ = Buf("wsc%d" % gid, local=False)
                wsc_keys[key] = (gid, b_sc_)
                op("sp", (lambda g: g.dma_start(out=wsc_d[gid, :, 0:nused], in_=ring[i][:, 0:nused])),
                   reads=[ring_b[i]], writes=[b_sc_], dsem=ring_st[i])
            return ring_b[i], ring[i]

        def wsrc(w_d, k0, nk, c0, ncol):
            return w_d[k0 * 128:(k0 + nk) * 128, c0:c0 + ncol].rearrange("(k p) n -> p k n", p=128)

        def dbg_dump(name, src_ap, reads, row0=0, col0=0):
            if name not in dbg_d:
                return
            dd = dbg_d[name]
            shp = src_ap.shape
            op("pool", lambda g: g.dma_start(out=dd[row0:row0 + shp[0], col0:col0 + shp[1]], in_=src_ap),
               reads=reads + [Buf("dbgl")], dsem=ds_dbg)

        def emit_mod_group(grp):
            rb, rt = ring_load([(0, KC * 512, wsrc(w_ada_d, 0, KC, grp * 512, 512))], cache=False)
            wv = rt[:, 0:KC * 512].rearrange("p (k n) -> p k n", k=KC)
            for cj in range(4):
                j = grp * 4 + cj
                pb, pt = ps()
                for kc in range(KC):
                    op("pe", (lambda g, pt=pt, kc=kc, cj=cj, wv=wv:
                              g.matmul(pt[:, 0:NSEQ], wv[:, kc, cj * 128:(cj + 1) * 128], scT[:, kc, :],
                                       start=(kc == 0), stop=(kc == KC - 1))),
                       reads=[rb, P_["scT"]], writes=[pb])
                op("dve", (lambda g, pt=pt, j=j:
                           g.tensor_scalar(out=modT[:, j, :], in0=pt[:, 0:NSEQ], scalar1=pv("b_ada", j, j + 1),
                                           scalar2=None, op0=ALU.add)),
                   reads=[pb, P_["prm"]], writes=[P_["modT"]])

        def emit_gm(which):
            gname, joff = (("norm1_g", 16), ("norm2_g", 64))[which]
            for b in range(NSEQ):
                op("dve", (lambda g, b=b:
                           g.scalar_tensor_tensor(out=gm[:, b, which, :], in0=modT[:, joff:joff + 16, b], scalar=1.0,
                                                  in1=pv(gname), op0=ALU.add, op1=ALU.mult)),
                   reads=[P_["modT"], P_["prm"]], writes=[P_["gm"]])

        for grp in range(8):
            emit_mod_group(grp)
        emit_gm(0)
        mod_pending = list(range(8, 24))

        def emit_mod_pending(n):
            for _ in range(min(n, len(mod_pending))):
                emit_mod_group(mod_pending.pop(0))
            if not mod_pending and not mod_done[0]:
                emit_gm(1)
                mod_done[0] = True
        mod_done = [False]
        if "modT" in dbg_d:
            dbg_dump("modT", modT[:].rearrange("p j b -> p (j b)"), [P_["modT"]])

        SH1, SC1, G1, SH2, SC2, G2 = 0, 16, 32, 48, 64, 80

        def norm_tmp(stk):
            return (sb("n_ss", [128, 8], F32, stk), sb("n_junk", [128, D], BF16, stk),
                    sb("n_xs4", [128, TPS, D], BF16, stk), Buf("ss"), Buf("junk"), Buf("xs4"))

        def norm_to_hT(tmp, src4, src_b, which, bseq, hT, hT_b, sub, col_off):
            ss, junk, xs4, b_ss, b_junk, b_xs = tmp
            for tt in range(TPS):
                op("act", (lambda g, tt=tt: g.activation(out=junk[:], in_=src4[:, tt, :], func=AF.Square,
                                                         accum_out=ss[:, tt:tt + 1])),
                   reads=[src_b], writes=[b_junk, b_ss])
            op("act", lambda g: g.activation(out=ss[:, 0:TPS], in_=ss[:, 0:TPS], func=AF.Sqrt, scale=1.0 / D,
                                             bias=epsT[:, 0:1]),
               reads=[b_ss, P_["epsT"]], writes=[b_ss])
            op("dve", lambda g: g.reciprocal(out=ss[:, 0:TPS], in_=ss[:, 0:TPS]), reads=[b_ss], writes=[b_ss])
            for tt in range(TPS):
                op("dve", (lambda g, tt=tt: g.tensor_scalar(out=xs4[:, tt, :], in0=src4[:, tt, :],
                                                            scalar1=ss[:, tt:tt + 1], scalar2=None, op0=ALU.mult)),
                   reads=[src_b, b_ss], writes=[b_xs])
            for kc in range(KC):
                pb, pt = ps()
                ptb = pt.bitcast(BF16)
                for tt in range(TPS):
                    op("pe", (lambda g, ptb=ptb, tt=tt, kc=kc:
                              g.transpose(ptb[:, tt * 128:(tt + 1) * 128], xs4[:, tt, kc * 128:(kc + 1) * 128],
                                          identb[:])),
                       reads=[b_xs, P_["identb"]], writes=[pb])
                dst = hT[:, kc, col_off + sub * SUBW: col_off + (sub + 1) * SUBW]
                sc_ap = gm[:, bseq, which, kc:kc + 1]
                sh_ap = modT[:, (SH1 if which == 0 else SH2) + kc, bseq:bseq + 1]
                if kc % 2 == 0:
                    op("act", (lambda g, dst=dst, ptb=ptb, sc_ap=sc_ap, sh_ap=sh_ap:
                               g.activation(out=dst, in_=ptb[:, 0:SUBW], func=AF.Identity, scale=sc_ap, bias=sh_ap)),
                       reads=[pb, P_["gm"], P_["modT"]], writes=[hT_b[sub][kc]])
                else:
                    op("dve", (lambda g, dst=dst, ptb=ptb, sc_ap=sc_ap, sh_ap=sh_ap:
                               g.tensor_scalar(out=dst, in0=ptb[:, 0:SUBW], scalar1=sc_ap, scalar2=sh_ap,
                                               op0=ALU.mult, op1=ALU.add)),
                       reads=[pb, P_["gm"], P_["modT"]], writes=[hT_b[sub][kc]])

        def down_proj(stk, wd, KCN, act_of, act_reads, gsel, bseq, src_rows_ap, dst_rows_ap, src_dram_b, dst_dram_b,
                      after_sub=None, tag="", xo=None):
            if xo is None:
                xo = sb("dp_xo" + tag, [128, TPS, D], F32, stk)[:]
            b_xo = Buf("xo")
            yTs = [sb("dp_yT%d" % i + tag, [128, SUBW], F32, stk) for i in range(8 if KCN <= 16 else 4)]
            b_yTs = [Buf("yT%d" % i) for i in range(len(yTs))]
            for sub in range(NSUB):
                op("sp", (lambda g, sub=sub:
                          g.dma_start(out=xo, in_=src_rows_ap(sub).rearrange("(t p) d -> p t d", p=128))),
                   reads=[src_dram_b(sub)], writes=[b_xo], dsem=ds_xo)
                if KCN * 512 <= RING_E:
                    groups = [(f0, 512, [(0, KCN)]) for f0 in range(0, D, 512)]
                else:
                    kh = KCN // 2
                    groups = [(f0, 256, [(0, kh), (kh, KCN - kh)]) for f0 in range(0, D, 256)]
                fi = 0
                pending = []
                for (f0, ncol, ksplits) in groups:
                    nf = ncol // 128
                    pbs = [ps(hold=True) for _ in range(nf)]
                    for si, (k0, nk) in enumerate(ksplits):
                        rb, rt = ring_load([(0, nk * ncol, wsrc(wd, k0, nk, f0, ncol))])
                        wv = rt[:, 0:nk * ncol].rearrange("p (k n) -> p k n", k=nk)
                        for cj in range(nf):
                            pb, pt = pbs[cj]
                            for kk in range(nk):
                                kc = k0 + kk
                                op("pe", (lambda g, pt=pt, wv=wv, kk=kk, cj=cj, kc=kc, sub=sub:
                                          g.matmul(pt[:, 0:SUBW], wv[:, kk, cj * 128:(cj + 1) * 128], act_of(kc, sub),
                                                   start=(kc == 0), stop=(kc == KCN - 1))),
                                   reads=[rb] + act_reads(kc, sub), writes=[pb])
                    this = []
                    for cj in range(nf):
                        f = (f0 // 128) + cj
                        pb, pt = pbs[cj]
                        yi = fi % len(yTs)
                        fi += 1
                        yT, b_yT = yTs[yi], b_yTs[yi]
                        op("act", (lambda g, yT=yT, pt=pt, f=f:
                                   g.activation(out=yT[:], in_=pt[:, 0:SUBW], func=AF.Copy,
                                                scale=modT[:, gsel + f, bseq:bseq + 1])),
                           reads=[pb, P_["modT"]], writes=[b_yT])
                        ps_release(pb)

                        def tr(yT=yT, b_yT=b_yT, f=f):
                            pb2, pt2 = ps()
                            for tt in range(TPS):
                                op("pe", (lambda g, tt=tt:
                                          g.transpose(pt2[:, tt * 128:(tt + 1) * 128], yT[:, tt * 128:(tt + 1) * 128],
                                                      ident_f)),
                                   reads=[b_yT, P_["cst"]], writes=[pb2])
                            op("dve", (lambda g:
                                       g.tensor_tensor(out=xo[:, :, f * 128:(f + 1) * 128],
                                                       in0=xo[:, :, f * 128:(f + 1) * 128],
                                                       in1=pt2[:, 0:SUBW].rearrange("p (t c) -> p t c", t=TPS),
                                                       op=ALU.add)),
                               reads=[pb2, b_xo], writes=[b_xo])
                        this.append(tr)
                    for t_ in pending:
                        t_()
                    pending = this
                for t_ in pending:
                    t_()
                op("sp", (lambda g, sub=sub:
                          g.dma_start(out=dst_rows_ap(sub).rearrange("(t p) d -> p t d", p=128), in_=xo)),
                   reads=[b_xo], writes=[dst_dram_b(sub)], dsem=ds_xo)
                if after_sub is not None:
                    after_sub(sub, xo, b_xo)

        def gemm_F(rb, wv, ncj, KCN, act_of, act_reads, evac, cj0=0):
            for cj in range(ncj):
                for sub in range(NSUB):
                    pb, pt = ps()
                    for kc in range(KCN):
                        op("pe", (lambda g, pt=pt, wv=wv, kc=kc, cj=cj, sub=sub:
                                  g.matmul(pt[:, 0:SUBW], wv[:, kc, (cj0 + cj) * 128:(cj0 + cj + 1) * 128],
                                           act_of(kc, sub), start=(kc == 0), stop=(kc == KCN - 1))),
                           reads=[rb] + act_reads(kc, sub), writes=[pb])
                    evac(cj, sub, pb, pt)

        dram_out_b = {}

        def dob(seq, m, sub):
            k = (seq, m, sub)
            if k not in dram_out_b:
                dram_out_b[k] = Buf("dout%d_%d_%d" % k, local=False)
            return dram_out_b[k]

        dram_x_b = Buf("xin", local=False)
        att_scale = 128.0 ** -0.5

        for seq in range(NSEQ):
            for m in range(NMT):
                tok0 = seq * S + m * MT
                mt_es = ExitStack()
                with mt_es:
                    hTflat = sb("hT", [128, KC * (MT + 2)], BF16, mt_es)
                    hT = hTflat[:].rearrange("p (k n) -> p k n", k=KC)
                    xo8 = None
                    if KC * (MT + 2) // 2 >= TPS * D:
                        xo8 = hTflat.bitcast(F32)[:, 0:TPS * D].rearrange("p (t d) -> p t d", t=TPS)
                    hT_b = [[Buf("hT%d_%d" % (s_, k)) for k in range(KC)] for s_ in range(NSUB)]
                    hT_halo_b = Buf("hThalo")
                    mg_es = ExitStack()
                    mg = sb("merged", [128, KC, MT], BF16, mg_es)
                    b_mg = [[Buf("mg%d_%d" % (f_, s_)) for s_ in range(NSUB)] for f_ in range(KC)]

                    def h_of(kc, sub, hT=hT):
                        return hT[:, kc, 2 + sub * SUBW: 2 + (sub + 1) * SUBW]

                    def h_reads(kc, sub, hT_b=hT_b):
                        return [hT_b[sub][kc]]

                    p_es = ExitStack()
                    with p_es:
                        x4 = sb("x4", [128, TPS, D], F32, p_es)
                        b_x4 = Buf("x4")
                        for sub in range(NSUB):
                            r0 = tok0 + sub * SUBW
                            op("sp", (lambda g, r0=r0: g.dma_start(
                                out=x4[:], in_=x_d[r0:r0 + SUBW, :].rearrange("(t p) d -> p t d", p=128))),
                               reads=[dram_x_b], writes=[b_x4], dsem=ds_x4)
                            if sub == 0:
                                ntmp = norm_tmp(p_es)
                            norm_to_hT(ntmp, x4, b_x4, 0, seq, hT, hT_b, sub, 2)
                        sch.barrier(skip=ring_ds + ring_st)
                    if "hT" in dbg_d and seq == 0:
                        for kc in range(KC):
                            for sub in range(NSUB):
                                dbg_dump("hT", h_of(kc, sub), [hT_b[sub][kc]], row0=kc * 128,
                                         col0=m * MT + sub * SUBW)

                    att_es = ExitStack()
                    with att_es:
                        qT = sb("qT", [128, NT, 8, 128], BF16, att_es)
                        b_qT = [[Buf("qT%d_%d" % (t_, h_)) for h_ in range(8)] for t_ in range(NT)]
                        qiT = sb("qiT", [128, 8, MT], BF16, att_es)
                        b_qiT = [[Buf("qiT%d_%d" % (p_, s_)) for s_ in range(NSUB)] for p_ in range(8)]
                        absw = sb("absw", [128, NT, NIH], F32, att_es)
                        sgnw = sb("sgnw", [128, NT, NIH], F32, att_es)
                        b_w = [Buf("widx%d" % t_) for t_ in range(NT)]
                        p1_es = ExitStack()
                        with p1_es:
                            sq = [sb("p1_sq%d" % i, [128, SUBW], F32, p1_es) for i in range(2)]
                            b_sq = [Buf("sq%d" % i) for i in range(2)]
                            rr = [sb("p1_rr%d" % i, [128, SUBW], F32, p1_es) for i in range(2)]
                            b_rr = [Buf("rr%d" % i) for i in range(2)]
                            cnt = [0]

                            qk_def = []

                            def qk_flush():
                                while qk_def:
                                    qk_def.pop(0)()

                            def qk_evac(pb, pt, gname, dst_ap, dst_bufs):
                                qk_flush()
                                i = cnt[0] % 2
                                cnt[0] += 1
                                ps_held.add(ps_b.index(pb))
                                op("act", (lambda g: g.activation(out=sq[i][:], in_=pt[:, 0:SUBW], func=AF.Square)),
                                   reads=[pb], writes=[b_sq[i]])

                                def part_b():
                                    pb2, pt2 = ps()
                                    op("pe", (lambda g: g.matmul(pt2[:, 0:SUBW], ones_f, sq[i][:], start=True, stop=True)),
                                       reads=[b_sq[i], P_["cst"]], writes=[pb2])
                                    op("act", (lambda g: g.activation(out=rr[i][:], in_=pt2[:, 0:SUBW], func=AF.Sqrt,
                                                                      scale=1.0 / 128, bias=epsT[:, 0:1])),
                                       reads=[pb2, P_["epsT"]], writes=[b_rr[i]])
                                    op("dve", (lambda g: g.reciprocal(out=rr[i][:], in_=rr[i][:])),
                                       reads=[b_rr[i]], writes=[b_rr[i]])
                                    op("dve", (lambda g: g.scalar_tensor_tensor(
                                        out=dst_ap, in0=pt[:, 0:SUBW].rearrange("p (t c) -> p t c", t=TPS)
                                        if len(dst_ap.shape) == 3 else pt[:, 0:SUBW],
                                        scalar=pv(gname), in1=rr[i][:].rearrange("p (t c) -> p t c", t=TPS)
                                        if len(dst_ap.shape) == 3 else rr[i][:], op0=ALU.mult, op1=ALU.mult)),
                                       reads=[pb, b_rr[i], P_["prm"]], writes=dst_bufs)
                                    ps_release(pb)
                                qk_def.append(part_b)

                            for qg in range(2):
                                rb, rt = ring_load([(0, KC * 512, wsrc(w_in_d, 0, KC, C_Q + qg * 512, 512))])
                                wv = rt[:, 0:KC * 512].rearrange("p (k n) -> p k n", k=KC)

                                def ev_q(cj, sub, pb, pt, qg=qg):
                                    h = qg * 4 + cj
                                    dst = qT[:, sub * TPS:(sub + 1) * TPS, h, :]
                                    qk_evac(pb, pt, "q_norm_g", dst, [b_qT[sub * TPS + t_][h] for t_ in range(TPS)])
                                gemm_F(rb, wv, 4, KC, h_of, h_reads, ev_q)
                            rb, rt = ring_load([(0, KC * 512, wsrc(w_in_d, 0, KC, C_K, 512))])
                            wv = rt[:, 0:KC * 512].rearrange("p (k n) -> p k n", k=KC)

                            def ev_k(cj, sub, pb, pt):
                                c0 = m * MT + sub * SUBW
                                qk_evac(pb, pt, "k_norm_g", kT[:, cj, c0:c0 + SUBW], [P_["kT"]])
                            gemm_F(rb, wv, 2, KC, h_of, h_reads, ev_k)
                            qk_flush()
                            for tt in range(NT):
                                pb, pt = ps()
                                sub, ti = divmod(tt, TPS)
                                for kc in range(KC):
                                    op("pe", (lambda g, pt=pt, kc=kc, sub=sub, ti=ti, wv=wv:
                                              g.matmul(pt[:, 0:256], hT[:, kc, 2 + sub * SUBW + ti * 128:
                                                                        2 + sub * SUBW + (ti + 1) * 128],
                                                       wv[:, kc, 256:512], start=(kc == 0), stop=(kc == KC - 1))),
                                       reads=[rb, hT_b[sub][kc]], writes=[pb])
                                ch = m * NT + tt
                                op("act", (lambda g, pt=pt, ch=ch:
                                           g.activation(out=Vc[:, ch, :, :].rearrange("p g d -> p (g d)"),
                                                        in_=pt[:, 0:256], func=AF.Copy)),
                                   reads=[pb], writes=[P_["Vc"]])
                            for qg in range(2):
                                rb, rt = ring_load([(0, KC * 512, wsrc(w_in_d, 0, KC, C_QI + qg * 512, 512))])
                                wv = rt[:, 0:KC * 512].rearrange("p (k n) -> p k n", k=KC)

                                def ev_qi(cj, sub, pb, pt, qg=qg):
                                    pr = qg * 4 + cj
                                    op("dve", (lambda g: g.tensor_copy(out=qiT[:, pr, sub * SUBW:(sub + 1) * SUBW],
                                                                       in_=pt[:, 0:SUBW])),
                                       reads=[pb], writes=[b_qiT[pr][sub]])
                                gemm_F(rb, wv, 4, KC, h_of, h_reads, ev_qi)
                            kdst = lambda half: (lambda rt_: rt_[:, 0:KC * 128].rearrange("p (k n) -> p k n", k=KC)
                                                 [:, :, half * 64:(half + 1) * 64])
                            rb, rt = ring_load([(0, 0, wsrc(w_in_d, 0, KC, C_KI, 64), kdst(0)),
                                                (0, 0, wsrc(w_in_d, 0, KC, C_KI, 64), kdst(1)),
                                                (KC * 128, KC * 16, wsrc(w_in_d, 0, KC, C_WI, 16))], nused=KC * 144)
                            wkk = rt[:, 0:KC * 128].rearrange("p (k n) -> p k n", k=KC)
                            wwi = rt[:, KC * 128:KC * 144].rearrange("p (k n) -> p k n", k=KC)
                            for sub in range(NSUB):
                                pb, pt = ps()
                                for kc in range(KC):
                                    op("pe", (lambda g, pt=pt, kc=kc, sub=sub:
                                              g.matmul(pt[:, 0:SUBW], wkk[:, kc, :], h_of(kc, sub),
                                                       start=(kc == 0), stop=(kc == KC - 1))),
                                       reads=[rb, hT_b[sub][kc]], writes=[pb])
                                c0 = m * MT + sub * SUBW
                                for half in range(2):
                                    op("act", (lambda g, pt=pt, c0=c0, half=half:
                                               g.activation(out=kiT[half * 64:(half + 1) * 64, half, c0:c0 + SUBW],
                                                            in_=pt[half * 64:(half + 1) * 64, 0:SUBW], func=AF.Copy)),
                                       reads=[pb], writes=[P_["kiT"]])
                            for tt in range(NT):
                                pb, pt = ps()
                                sub, ti = divmod(tt, TPS)
                                for kc in range(KC):
                                    op("pe", (lambda g, pt=pt, kc=kc, sub=sub, ti=ti:
                                              g.matmul(pt[:, 0:NIH], hT[:, kc, 2 + sub * SUBW + ti * 128:
                                                                        2 + sub * SUBW + (ti + 1) * 128],
                                                       wwi[:, kc, :], start=(kc == 0), stop=(kc == KC - 1))),
                                       reads=[rb, hT_b[sub][kc]], writes=[pb])
                                op("act", (lambda g, pt=pt, tt=tt:
                                           g.activation(out=absw[:, tt, :], in_=pt[:, 0:NIH], func=AF.Abs)),
                                   reads=[pb], writes=[b_w[tt]])
                                op("act", (lambda g, pt=pt, tt=tt:
                                           g.activation(out=sgnw[:, tt, :], in_=pt[:, 0:NIH], func=AF.Sign)),
                                   reads=[pb], writes=[b_w[tt]])
                            sch.barrier(skip=ring_ds + ring_st)
                        if seq == 0:
                            if "qT" in dbg_d:
                                for tt in range(NT):
                                    dbg_dump("qT", qT[:, tt, :, :].rearrange("p h t -> p (h t)"), b_qT[tt],
                                             col0=(m * NT + tt) * 1024)
                            if "qiT" in dbg_d:
                                for pr in range(8):
                                    for sub in range(NSUB):
                                        dbg_dump("qiT", qiT[:, pr, sub * SUBW:(sub + 1) * SUBW], [b_qiT[pr][sub]],
                                                 row0=pr * 128, col0=m * MT + sub * SUBW)
                            if "absw" in dbg_d and m == 0:
                                dbg_dump("absw", absw[:].rearrange("p t h -> p (t h)"), b_w)

                        p2_es = ExitStack()
                        with p2_es:
                            sc = [sb("sc%d" % i, [128, S], F32, p2_es) for i in range(2)]
                            b_sc = [Buf("sc%d" % i) for i in range(2)]
                            Rr = [sb("Rr%d" % i, [128, 512], BF16, p2_es) for i in range(3)]
                            b_Rr = [Buf("Rr%d" % i) for i in range(3)]
                            Dg = sb("Dg", [128, NIH, 128], BF16, p2_es)
                            b_Dg = Buf("Dg")
                            mTS = sb("mTS", [128, S], BF16, p2_es)
                            b_mTS = Buf("mTS")
                            junkb, b_junkb = mTS, b_mTS
                            mdbg = sb("mdbg", [128, S], F32, p2_es) if "mTS" in dbg_d else None
                            b_mdbg = Buf("mdbg")
                            mT = [sb("mT%d" % i, [128, NCH, 128], BF16, p2_es) for i in range(2)]
                            b_mT = [Buf("mT%d" % i) for i in range(2)]
                            pT = [sb("pT%d" % i, [128, 512], BF16, p2_es) for i in range(4)]
                            b_pT = [Buf("pT%d" % i) for i in range(4)]
                            rinv = sb("rinv", [128, 512], F32, p2_es)
                            b_rinv = Buf("rinv")
                            bs = sb("bs", [128, 8], F32, p2_es)
                            b_bs = Buf("bs")
                            wtab = sb("wtab", [128, NBIS], F32, p2_es)
                            nwtab = sb("nwtab", [128, NBIS], F32, p2_es)
                            b_wtab = Buf("wtab")
                            b_bs6 = Buf("bs6")

                            ctr = {"ri": 0, "pi": 0}
                            if seq == 0 and m == 0:
                                build_nc.p2_slack = nc.sbuf_bytes_remaining

                            def tile_info(tl):
                                qg_ = m * NT + tl
                                nk = (qg_ + 1) * 128
                                return qg_, nk, (nk > TOPK)

                            def score_steps(tl):
                                qg_, nk, _ = tile_info(tl)
                                scb, b_scb = sc[tl % 2], b_sc[tl % 2]
                                steps = []

                                def build_dg():
                                    for h in range(NIH):
                                        op("dve", (lambda g, h=h:
                                                   g.tensor_scalar(out=Dg[:, h, :], in0=identb[:],
                                                                   scalar1=sgnw[:, tl, h:h + 1], scalar2=None, op0=ALU.mult)),
                                           reads=[P_["identb"], b_w[tl]], writes=[b_Dg])
                                steps.append(build_dg)
                                nkb = (nk + 511) // 512
                                for kb in range(nkb):
                                    ncol = min(512, nk - kb * 512)
                                    stt = {}

                                    def front(h, kb=kb, ncol=ncol, stt=stt):
                                        pr, half = divmod(h, 2)
                                        pb, pt = ps()
                                        op("pe", (lambda g: g.matmul(pt[:, 0:ncol], qiT[:, pr, tl * 128:(tl + 1) * 128],
                                                                     kiT[:, half, kb * 512:kb * 512 + ncol],
                                                                     start=True, stop=True)),
                                           reads=[b_qiT[pr][tl // TPS], P_["kiT"]], writes=[pb])
                                        r_i = ctr["ri"] % 3
                                        ctr["ri"] += 1
                                        if h % 3 == 2:
                                            op("dve", (lambda g: g.tensor_scalar(out=Rr[r_i][:, 0:ncol], in0=pt[:, 0:ncol],
                                                                                 scalar1=absw[:, tl, h:h + 1], scalar2=0.0,
                                                                                 op0=ALU.mult, op1=ALU.max)),
                                               reads=[pb, b_w[tl]], writes=[b_Rr[r_i]])
                                        else:
                                            op("act", (lambda g: g.activation(out=Rr[r_i][:, 0:ncol], in_=pt[:, 0:ncol],
                                                                              func=AF.Relu, scale=absw[:, tl, h:h + 1])),
                                               reads=[pb, b_w[tl]], writes=[b_Rr[r_i]])
                                        stt[h] = r_i

                                    def back(h, ncol=ncol, stt=stt):
                                        r_i = stt.pop(h)
                                        pba, pta = stt["acc"]
                                        op("pe", (lambda g: g.matmul(pta[:, 0:ncol], Dg[:, h, :], Rr[r_i][:, 0:ncol],
                                                                     start=(h == 0), stop=(h == NIH - 1))),
                                           reads=[b_Dg, b_Rr[r_i]], writes=[pba])
                                    LA_ = 2

                                    def mk(h, kb=kb, ncol=ncol, stt=stt, front=front, back=back):
                                        def f():
                                            if h == 0:
                                                stt["acc"] = ps(hold=True)
                                                for k_ in range(LA_):
                                                    front(k_)
                                            if h + LA_ < NIH:
                                                front(h + LA_)
                                            back(h)
                                            if h == NIH - 1:
                                                pba, pta = stt["acc"]
                                                dst = scb[:, kb * 512:kb * 512 + ncol]
                                                op("act", (lambda g: g.activation(out=dst, in_=pta[:, 0:ncol], func=AF.Copy)),
                                                   reads=[pba], writes=[b_scb])
                                                ps_release(pba)
                                                if kb == nkb - 1:
                                                    dg = scb[:, qg_ * 128:(qg_ + 1) * 128]
                                                    op("dve", (lambda g: g.tensor_tensor(out=dg, in0=dg, in1=negbig,
                                                                                         op=ALU.add)),
                                                       reads=[b_scb, P_["cst"]], writes=[b_scb])
                                                    if "sc" in dbg_d and seq == 0:
                                                        dbg_dump("sc", scb[:, 0:nk], [b_scb], row0=qg_ * 128)
                                        return f
                                    for h in range(NIH):
                                        steps.append(mk(h))
                                return steps

                            def emit_bisect_init(tl):
                                qg_, nk, _ = tile_info(tl)
                                scb, b_scb = sc[tl % 2], b_sc[tl % 2]
                                scv = scb[:, 0:nk]
                                op("dve", (lambda g: g.tensor_reduce(out=bs[:, 0:1], in_=scb[:, 0:TOPK],
                                                                     axis=AX.X, op=ALU.min)),
                                   reads=[b_scb], writes=[b_bs])
                                op("dve", (lambda g: g.tensor_reduce(out=bs[:, 4:5], in_=scv, axis=AX.X, op=ALU.max)),
                                   reads=[b_scb], writes=[b_bs])
                                op("dve", (lambda g: g.tensor_tensor(out=bs[:, 5:6], in0=bs[:, 4:5], in1=bs[:, 0:1],
                                                                     op=ALU.subtract)),
                                   reads=[b_bs], writes=[b_bs])
                                op("dve", (lambda g: g.tensor_scalar(out=wtab[:], in0=pow2, scalar1=bs[:, 5:6],
                                                                     scalar2=None, op0=ALU.mult)),
                                   reads=[b_bs, P_["cst"]], writes=[b_wtab])
                                op("dve", (lambda g: g.tensor_scalar(out=nwtab[:], in0=pow2, scalar1=bs[:, 5:6],
                                                                     scalar2=-1.0, op0=ALU.mult, op1=ALU.mult)),
                                   reads=[b_bs, P_["cst"]], writes=[b_wtab])
                                op("dve", (lambda g: g.tensor_tensor(out=bs[:, 1:2], in0=bs[:, 0:1], in1=wtab[:, 0:1],
                                                                     op=ALU.add)),
                                   reads=[b_bs, b_wtab], writes=[b_bs])

                            def emit_bisect_iter(tl, it):
                                qg_, nk, _ = tile_info(tl)
                                scb, b_scb = sc[tl % 2], b_sc[tl % 2]
                                scv = scb[:, 0:nk]
                                if it == 0:
                                    emit_bisect_init(tl)
                                if it % 2 == 0 or not ACT_COUNT:
                                    op("dve", (lambda g:
                                               g.tensor_scalar(out=junkb[:, 0:nk], in0=scv, scalar1=bs[:, 1:2],
                                                               scalar2=None, op0=ALU.is_ge, op1=ALU.add,
                                                               accum_out=bs[:, 2:3])),
                                       reads=[b_scb, b_bs], writes=[b_junkb, b_bs])
                                    op("dve", (lambda g:
                                               g.tensor_scalar(out=bs[:, 3:4], in0=bs[:, 2:3], scalar1=float(TOPK) - 0.5,
                                                               scalar2=wtab[:, it:it + 1], op0=ALU.is_ge, op1=ALU.mult)),
                                       reads=[b_bs, b_wtab], writes=[b_bs])
                                else:
                                    op("act", (lambda g:
                                               g.activation(out=junkb[:, 0:nk], in_=scv, func=AF.Sign, scale=-1.0,
                                                            bias=bs[:, 1:2], accum_out=bs[:, 6:7])),
                                       reads=[b_scb, b_bs], writes=[b_junkb, b_bs6])
                                    op("dve", (lambda g:
                                               g.tensor_scalar(out=bs[:, 3:4], in0=bs[:, 6:7],
                                                               scalar1=float(nk - 2 * TOPK) + 0.5,
                                                               scalar2=wtab[:, it:it + 1], op0=ALU.is_le, op1=ALU.mult)),
                                       reads=[b_bs6, b_wtab], writes=[b_bs])
                                nxt_w = it + 1 if it + 1 < NBIS else it
                                op("dve", (lambda g:
                                           g.scalar_tensor_tensor(out=bs[:, 1:2], in0=bs[:, 3:4],
                                                                  scalar=nwtab[:, nxt_w:nxt_w + 1], in1=bs[:, 1:2],
                                                                  op0=ALU.add, op1=ALU.add)),
                                   reads=[b_bs, b_wtab], writes=[b_bs])

                            def emit_masks(tl):
                                qg_, nk, _ = tile_info(tl)
                                si = tl % 2
                                scv = sc[tl % 2][:, 0:nk]
                                op("dve", (lambda g, scv=scv, nk=nk:
                                           g.tensor_scalar(out=mTS[:, 0:nk], in0=scv, scalar1=bs[:, 1:2],
                                                           scalar2=None, op0=ALU.is_ge)),
                                   reads=[b_sc[tl % 2], b_bs], writes=[b_mTS])
                                if "mTS" in dbg_d and seq == 0:
                                    op("dve", (lambda g, nk=nk: g.tensor_copy(out=mdbg[:, 0:nk], in_=mTS[:, 0:nk])),
                                       reads=[b_mTS], writes=[b_mdbg])
                                    dbg_dump("mTS", mdbg[:, 0:nk], [b_mdbg], row0=qg_ * 128)
                                for j0 in range(0, qg_ + 1, 8):
                                    nj = min(8, qg_ + 1 - j0)
                                    pb, pt = ps()
                                    ptb = pt.bitcast(BF16)
                                    for jj in range(nj):
                                        j = j0 + jj
                                        op("pe", (lambda g, ptb=ptb, jj=jj, j=j:
                                                  g.transpose(ptb[:, jj * 128:(jj + 1) * 128],
                                                              mTS[:, j * 128:(j + 1) * 128], identb[:])),
                                           reads=[b_mTS, P_["identb"]], writes=[pb])
                                    op("act", (lambda g, ptb=ptb, j0=j0, nj=nj, si=si:
                                               g.activation(out=mT[si][:, j0:j0 + nj, :].rearrange("p j t -> p (j t)"),
                                                            in_=ptb[:, 0:nj * 128], func=AF.Copy)),
                                       reads=[pb], writes=[b_mT[si]])

                            def att_steps(tl):
                                qg_, nk, select = tile_info(tl)
                                si = tl % 2
                                pairs = [(kvg, j) for kvg in range(2) for j in range(qg_ + 1)]
                                st = {}
                                LA = 2

                                def front(i):
                                    kvg, j = pairs[i]
                                    qrhs = qT[:, tl, kvg * 4:(kvg + 1) * 4, :].rearrange("p h t -> p (h t)")
                                    qb_reads = [b_qT[tl][kvg * 4 + hh] for hh in range(4)]
                                    pbs_, pts_ = ps()
                                    op("pe", (lambda g: g.matmul(pts_[:, 0:512], kT[:, kvg, j * 128:(j + 1) * 128], qrhs,
                                                                 start=True, stop=True)),
                                       reads=[P_["kT"]] + qb_reads, writes=[pbs_])
                                    p_i = ctr["pi"] % 4
                                    ctr["pi"] += 1
                                    op("act", (lambda g: g.activation(out=pT[p_i][:], in_=pts_[:, 0:512], func=AF.Exp,
                                                                      scale=att_scale)),
                                       reads=[pbs_], writes=[b_pT[p_i]])
                                    st[i] = p_i

                                def back(i):
                                    kvg, j = pairs[i]
                                    p_i = st.pop(i)
                                    qrhs = qT[:, tl, kvg * 4:(kvg + 1) * 4, :].rearrange("p h t -> p (h t)")
                                    qb_reads = [b_qT[tl][kvg * 4 + hh] for hh in range(4)]
                                    if j == 0:
                                        st["o%d" % kvg] = ps(hold=True)
                                        st["r%d" % kvg] = ps(hold=True)
                                    pbo, pto = st["o%d" % kvg]
                                    pbr, ptr = st["r%d" % kvg]
                                    if select or j == qg_:
                                        if select:
                                            msk, mreads = mT[si][:, j, :], [b_mT[si]]
                                        else:
                                            msk, mreads = trib[:], [P_["trib"]]
                                        op(MASK_ENG, (lambda g:
                                                   g.tensor_tensor(out=pT[p_i][:].rearrange("p (h t) -> p h t", h=4),
                                                                   in0=pT[p_i][:].rearrange("p (h t) -> p h t", h=4),
                                                                   in1=msk.unsqueeze(1).to_broadcast([128, 4, 128]),
                                                                   op=ALU.mult)),
                                           reads=[b_pT[p_i]] + mreads, writes=[b_pT[p_i]])
                                    op("pe", (lambda g: g.matmul(pto[:, 0:512], Vc[:, j, kvg, :], pT[p_i][:],
                                                                 start=(j == 0), stop=(j == qg_))),
                                       reads=[P_["Vc"], b_pT[p_i]], writes=[pbo])
                                    op("pe", (lambda g: g.matmul(ptr[:, 0:512], onesb[:], pT[p_i][:],
                                                                 start=(j == 0), stop=(j == qg_))),
                                       reads=[P_["onesb"], b_pT[p_i]], writes=[pbr])
                                    if j == qg_:
                                        op("dve", (lambda g: g.reciprocal(out=rinv[:], in_=ptr[:, 0:512])),
                                           reads=[pbr], writes=[b_rinv])
                                        op("dve", (lambda g: g.tensor_tensor(out=qrhs, in0=pto[:, 0:512], in1=rinv[:],
                                                                             op=ALU.mult)),
                                           reads=[pbo, b_rinv], writes=qb_reads)
                                        ps_release(pbo)
                                        ps_release(pbr)

                                steps = []
                                n = len(pairs)

                                def mk(i):
                                    def f():
                                        if i == 0:
                                            for k in range(min(LA, n)):
                                                front(k)
                                        if i + LA < n:
                                            front(i + LA)
                                        back(i)
                                    return f
                                for i in range(n):
                                    steps.append(mk(i))
                                return steps

                            def run_merged(*lists):
                                lists = [l for l in lists if l]
                                pos = [0] * len(lists)
                                while True:
                                    best, bi = None, None
                                    for i, l in enumerate(lists):
                                        if pos[i] < len(l):
                                            frac = (pos[i] + 0.5) / len(l)
                                            if best is None or frac < best:
                                                best, bi = frac, i
                                    if bi is None:
                                        break
                                    lists[bi][pos[bi]]()
                                    pos[bi] += 1

                            def sel(tl):
                                return tl < NT and tile_info(tl)[2]

                            def bis_steps(tl):
                                return [(lambda it=it: emit_bisect_iter(tl, it)) for it in range(NBIS)]

                            if sel(0):
                                run_merged(score_steps(0))
                                run_merged(bis_steps(0), score_steps(1) if sel(1) else [])
                                emit_masks(0)
                            elif sel(1):
                                run_merged(score_steps(1))
                            for tl in range(NT):
                                run_merged(bis_steps(tl + 1) if sel(tl + 1) else [], att_steps(tl),
                                           score_steps(tl + 2) if sel(tl + 2) else [])
                                if sel(tl + 1):
                                    emit_masks(tl + 1)
                                emit_mod_pending((16 + NT - 1) // NT if tl < NT - 1 else 16)

                            sch.barrier(skip=ring_ds + ring_st)
                        if "yb" in dbg_d and seq == 0:
                            for tt in range(NT):
                                dbg_dump("yb", qT[:, tt, :, :].rearrange("p h t -> p (h t)"), b_qT[tt],
                                         col0=(m * NT + tt) * 1024)

                        p3_es = ExitStack()
                        with p3_es:
                            sg = [sb("sg%d" % i, [128, SUBW], F32, p3_es) for i in range(2)]
                            b_sg = [Buf("sg%d" % i) for i in range(2)]
                            tmpm = [sb("tmpm%d" % i, [128, SUBW], BF16, p3_es) for i in range(2)]
                            b_tmpm = [Buf("tmpm%d" % i) for i in range(2)]
                            gi = [0]

                            def gated_proj(w_g_col, w_p_d, KCP, act_of, act_reads, accumulate):
                                for nb in range(8):
                                    rbg, rtg = ring_load([(0, KC * 256, wsrc(w_in_d, 0, KC, w_g_col + nb * 256, 256)),
                                                          (KC * 256, KCP * 256, wsrc(w_p_d, 0, KCP, nb * 256, 256))])
                                    wvg = rtg[:, 0:KC * 256].rearrange("p (k n) -> p k n", k=KC)
                                    rbp = rbg
                                    wvp = rtg[:, KC * 256:(KC + KCP) * 256].rearrange("p (k n) -> p k n", k=KCP)
                                    for cj in range(2):
                                        f = nb * 2 + cj
                                        for sub in range(NSUB):
                                            i = gi[0] % 2
                                            gi[0] += 1
                                            pbg, ptg = ps()
                                            for kc in range(KC):
                                                op("pe", (lambda g, ptg=ptg, kc=kc, cj=cj, sub=sub, wvg=wvg:
                                                          g.matmul(ptg[:, 0:SUBW], wvg[:, kc, cj * 128:(cj + 1) * 128],
                                                                   h_of(kc, sub), start=(kc == 0), stop=(kc == KC - 1))),
                                                   reads=[rbg, hT_b[sub][kc]], writes=[pbg])
                                            op("act", (lambda g, ptg=ptg, i=i:
                                                       g.activation(out=sg[i][:], in_=ptg[:, 0:SUBW], func=AF.Sigmoid)),
                                               reads=[pbg], writes=[b_sg[i]])
                                            pbp, ptp = ps()
                                            for kc in range(KCP):
                                                op("pe", (lambda g, ptp=ptp, kc=kc, cj=cj, sub=sub, wvp=wvp:
                                                          g.matmul(ptp[:, 0:SUBW], wvp[:, kc, cj * 128:(cj + 1) * 128],
                                                                   act_of(kc, sub), start=(kc == 0),
                                                                   stop=(kc == KCP - 1))),
                                                   reads=[rbp] + act_reads(kc, sub), writes=[pbp])
                                            dst = mg[:, f, sub * SUBW:(sub + 1) * SUBW]
                                            if not accumulate:
                                                op("dve", (lambda g, dst=dst, ptp=ptp, i=i:
                                                           g.tensor_tensor(out=dst, in0=ptp[:, 0:SUBW], in1=sg[i][:],
                                                                           op=ALU.mult)),
                                                   reads=[pbp, b_sg[i]], writes=[b_mg[f][sub]])
                                            else:
                                                op("dve", (lambda g, ptp=ptp, i=i:
                                                           g.tensor_tensor(out=tmpm[i][:], in0=ptp[:, 0:SUBW],
                                                                           in1=sg[i][:], op=ALU.mult)),
                                                   reads=[pbp, b_sg[i]], writes=[b_tmpm[i]])
                                                op("dve", (lambda g, dst=dst, i=i:
                                                           g.tensor_tensor(out=dst, in0=dst, in1=tmpm[i][:], op=ALU.add)),
                                                   reads=[b_tmpm[i], b_mg[f][sub]], writes=[b_mg[f][sub]])

                            def yb_of(kc, sub):
                                return qT[:, sub * TPS:(sub + 1) * TPS, kc, :]

                            def yb_reads(kc, sub):
                                return [b_qT[sub * TPS + t_][kc] for t_ in range(TPS)]
                            gated_proj(C_GB, w_pb_d, 8, yb_of, yb_reads, False)
                            sch.barrier(skip=ring_ds + ring_st)
                    if "mB" in dbg_d and seq == 0:
                        for f in range(KC):
                            for sub in range(NSUB):
                                dbg_dump("mB", mg[:, f, sub * SUBW:(sub + 1) * SUBW], [b_mg[f][sub]], row0=f * 128,
                                         col0=m * MT + sub * SUBW)

                    a_es = ExitStack()
                    with a_es:
                        uT = sb("uT", [128, 8, MT], BF16, a_es)
                        b_uT = [[Buf("uT%d_%d" % (c_, t_)) for t_ in range(NT)] for c_ in range(8)]
                        vv = sb("vv", [128, NT, A_W], BF16, a_es)
                        b_vv = [Buf("vv%d" % t_) for t_ in range(NT)]
                        vss = sb("vss", [128, NT], F32, a_es)
                        b_vss = Buf("vss")
                        vjunk = sb("vjunk", [128, A_W], BF16, a_es)
                        b_vjunk = Buf("vjunk")
                        tmpa = [sb("tmpa%d" % i, [128, 4, 128], F32, a_es) for i in range(2)]
                        b_tmpa = [Buf("tmpa%d" % i) for i in range(2)]
                        for vg in range(2):
                            rb, rt = ring_load([(0, KC * 512, wsrc(w_in_d, 0, KC, C_VA + vg * 512, 512))])
                            wv = rt[:, 0:KC * 512].rearrange("p (k n) -> p k n", k=KC)
                            for tt in range(NT):
                                pb, pt = ps()
                                sub, ti = divmod(tt, TPS)
                                for kc in range(KC):
                                    op("pe", (lambda g, pt=pt, kc=kc, sub=sub, ti=ti, wv=wv:
                                              g.matmul(pt[:, 0:512], hT[:, kc, 2 + sub * SUBW + ti * 128:
                                                                        2 + sub * SUBW + (ti + 1) * 128],
                                                       wv[:, kc, :], start=(kc == 0), stop=(kc == KC - 1))),
                                       reads=[rb, hT_b[sub][kc]], writes=[pb])
                                op("act", (lambda g, pt=pt, tt=tt, vg=vg:
                                           g.activation(out=vv[:, tt, vg * 512:(vg + 1) * 512], in_=pt[:, 0:512],
                                                        func=AF.Gelu_apprx_tanh)),
                                   reads=[pb], writes=[b_vv[tt]])
                        for tt in range(NT):
                            op("act", (lambda g, tt=tt: g.activation(out=vjunk[:], in_=vv[:, tt, :], func=AF.Square,
                                                                     accum_out=vss[:, tt:tt + 1])),
                               reads=[b_vv[tt]], writes=[b_vjunk, b_vss])
                        op("act", lambda g: g.activation(out=vss[:], in_=vss[:], func=AF.Sqrt, scale=1.0 / A_W,
                                                         bias=epsT[:, 0:1]),
                           reads=[b_vss, P_["epsT"]], writes=[b_vss])
                        op("dve", lambda g: g.reciprocal(out=vss[:], in_=vss[:]), reads=[b_vss], writes=[b_vss])
                        for tt in range(NT):
                            op("dve", (lambda g, tt=tt: g.tensor_scalar(out=vv[:, tt, :], in0=vv[:, tt, :],
                                                                        scalar1=vss[:, tt:tt + 1], scalar2=None,
                                                                        op0=ALU.mult)),
                               reads=[b_vv[tt], b_vss], writes=[b_vv[tt]])
                        for ug in range(2):
                            rb, rt = ring_load([(0, KC * 512, wsrc(w_in_d, 0, KC, C_U + ug * 512, 512))])
                            wv = rt[:, 0:KC * 512].rearrange("p (k n) -> p k n", k=KC)

                            def ev_u(cj, sub, pb, pt, ug=ug):
                                c = ug * 4 + cj
                                op("act", (lambda g: g.activation(out=uT[:, c, sub * SUBW:(sub + 1) * SUBW],
                                                                  in_=pt[:, 0:SUBW], func=AF.Gelu_apprx_tanh)),
                                   reads=[pb], writes=[b_uT[c][sub * TPS + t_] for t_ in range(TPS)])
                            gemm_F(rb, wv, 4, KC, h_of, h_reads, ev_u)
                        ai = 0
                        for tt in range(NT):
                            for g4 in range(2):
                                pb, pt = ps()
                                for gg in range(4):
                                    gi_ = g4 * 4 + gg
                                    op("pe", (lambda g, pt=pt, gg=gg, gi_=gi_, tt=tt:
                                              g.matmul(pt[:, gg * 128:(gg + 1) * 128], vv[:, tt, gi_ * 128:(gi_ + 1) * 128],
                                                       wspT[:, gi_, :], start=True, stop=True)),
                                       reads=[b_vv[tt], P_["wspT"]], writes=[pb])
                                a_i = ai % 2
                                ai += 1
                                for gg in range(4):
                                    gi_ = g4 * 4 + gg
                                    op("dve", (lambda g, pt=pt, gg=gg, gi_=gi_, a_i=a_i:
                                               g.scalar_tensor_tensor(out=tmpa[a_i][:, gg, :],
                                                                      in0=pt[:, gg * 128:(gg + 1) * 128],
                                                                      scalar=pv("v_norm_g", gi_, gi_ + 1),
                                                                      in1=pv("bsp_rep", gi_ * 128, (gi_ + 1) * 128),
                                                                      op0=ALU.mult, op1=ALU.add)),
                                       reads=[pb, P_["prm"]], writes=[b_tmpa[a_i]])
                                uview = uT[:, g4 * 4:(g4 + 1) * 4, tt * 128:(tt + 1) * 128]
                                op("dve", (lambda g, uview=uview, a_i=a_i:
                                           g.tensor_tensor(out=uview, in0=uview, in1=tmpa[a_i][:], op=ALU.mult)),
                                   reads=[b_tmpa[a_i]] + [b_uT[g4 * 4 + gg][tt] for gg in range(4)],
                                   writes=[b_uT[g4 * 4 + gg][tt] for gg in range(4)])
                        if "ya" in dbg_d and seq == 0:
                            for c in range(8):
                                dbg_dump("ya", uT[:, c, :], b_uT[c], row0=c * 128, col0=m * MT)
                        p5_es = ExitStack()
                        with p5_es:
                            sg = [sb("sg5_%d" % i, [128, SUBW], F32, p5_es) for i in range(2)]
                            b_sg = [Buf("sg%d" % i) for i in range(2)]
                            tmpm = [sb("tmpm5_%d" % i, [128, SUBW], BF16, p5_es) for i in range(2)]
                            b_tmpm = [Buf("tmpm%d" % i) for i in range(2)]

                            def ya_of(kc, sub):
                                return uT[:, kc, sub * SUBW:(sub + 1) * SUBW]

                            def ya_reads(kc, sub):
                                return [b_uT[kc][sub * TPS + t_] for t_ in range(TPS)]
                            gated_proj(C_GA, w_pa_d, 8, ya_of, ya_reads, True)
                            sch.barrier(skip=ring_ds + ring_st)
                    if "merged" in dbg_d and seq == 0:
                        for f in range(KC):
                            for sub in range(NSUB):
                                dbg_dump("merged", mg[:, f, sub * SUBW:(sub + 1) * SUBW], [b_mg[f][sub]], row0=f * 128,
                                         col0=m * MT + sub * SUBW)

                    p6_es = ExitStack()
                    with p6_es:
                        def mg_of(kc, sub):
                            return mg[:, kc, sub * SUBW:(sub + 1) * SUBW]

                        def mg_reads(kc, sub):
                            return [b_mg[kc][sub]]
                        h2_b = [[Buf("h2T%d_%d" % (s_, k)) for k in range(KC)] for s_ in range(NSUB)]

                        ntmp6 = norm_tmp(p6_es)

                        def after6(sub, xo, b_xo):
                            norm_to_hT(ntmp6, xo, b_xo, 1, seq, hT, h2_b, sub, 2)
                        op("dve", lambda g: g.tensor_copy(out=hT[:, :, 0:2], in_=halo[:]),
                           reads=[P_["halo"]], writes=[hT_halo_b])
                        down_proj(p6_es, w_out_d, KC, mg_of, mg_reads, G1, seq,
                                  lambda sub: x_d[tok0 + sub * SUBW: tok0 + (sub + 1) * SUBW, :],
                                  lambda sub: out_d[tok0 + sub * SUBW: tok0 + (sub + 1) * SUBW, :],
                                  lambda sub: dram_x_b, lambda sub: dob(seq, m, sub), after_sub=after6, tag="6")
                        op("dve", lambda g: g.tensor_copy(out=halo[:], in_=hT[:, :, MT:MT + 2]),
                           reads=[h2_b[NSUB - 1][k] for k in range(KC)], writes=[P_["halo"]])
                        sch.barrier(skip=ring_ds + ring_st)
                    mg_es.close()
                    if m == NMT - 1:
                        op("dve", lambda g: g.memset(halo[:], 0.0), writes=[P_["halo"]])
                    if "h2T" in dbg_d and seq == 0:
                        for kc in range(KC):
                            for sub in range(NSUB):
                                dbg_dump("h2T", h_of(kc, sub), [h2_b[sub][kc]], row0=kc * 128,
                                         col0=m * MT + sub * SUBW)

                    f_es = ExitStack()
                    with f_es:
                        gT = sb("gT", [128, FC, MT], BF16, f_es)
                        b_gT = [[Buf("gT%d_%d" % (c_, s_)) for s_ in range(NSUB)] for c_ in range(FC)]
                        p7_es = ExitStack()
                        with p7_es:
                            ya_ = [sb("cy%d" % i, [128, 512], F32, p7_es) for i in range(4)]
                            b_ya = [Buf("cy%d" % i) for i in range(4)]
                            b_yab = [Buf("cyb%d" % i) for i in range(4)]
                            yi = 0
                            W = SUBW
                            for c2 in range(FC // 2):
                                rb, rt = ring_load([(0, KC * 256, wsrc(w_up_d, 0, KC, c2 * 256, 256)),
                                                    (KC * 256, KC * 256, wsrc(w_up_d, 0, KC, DFF + c2 * 256, 256))])
                                wva = rt[:, 0:KC * 256].rearrange("p (k n) -> p k n", k=KC)
                                wvb = rt[:, KC * 256:KC * 512].rearrange("p (k n) -> p k n", k=KC)
                                for cc in range(2):
                                    c = c2 * 2 + cc
                                    for sub in range(NSUB):
                                        res = []
                                        for ab, wv_ in ((0, wva), (1, wvb)):
                                            ch = c + ab * FC
                                            pb, pt = ps()
                                            for kc in range(KC):
                                                op("pe", (lambda g, pt=pt, wv_=wv_, kc=kc, cc=cc, sub=sub:
                                                          g.matmul(pt[:, 0:W], wv_[:, kc, cc * 128:(cc + 1) * 128],
                                                                   h_of(kc, sub), start=(kc == 0), stop=(kc == KC - 1))),
                                                   reads=[rb, h2_b[sub][kc]], writes=[pb])
                                            y_i = yi % 4
                                            yi += 1
                                            yt, b_y, b_yb = ya_[y_i], b_ya[y_i], b_yab[y_i]
                                            cw = lambda j, ch=ch: pv("conv_w", j * 88 + ch, j * 88 + ch + 1)
                                            cb = pv("conv_b", ch, ch + 1)
                                            ux = uhx[:, ch, :]
                                            op("act", (lambda g, pt=pt, yt=yt, cw=cw, cb=cb:
                                                       g.activation(out=yt[:, 2:W], in_=pt[:, 2:W], func=AF.Identity,
                                                                    scale=cw(2), bias=cb)),
                                               reads=[pb, P_["prm"]], writes=[b_y])
                                            op("dve", (lambda g, pt=pt, yt=yt, cw=cw:
                                                       g.scalar_tensor_tensor(out=yt[:, 2:W], in0=pt[:, 1:W - 1], scalar=cw(1),
                                                                              in1=yt[:, 2:W], op0=ALU.mult, op1=ALU.add)),
                                               reads=[pb, b_y, P_["prm"]], writes=[b_y])
                                            op("dve", (lambda g, pt=pt, yt=yt, cw=cw:
                                                       g.scalar_tensor_tensor(out=yt[:, 2:W], in0=pt[:, 0:W - 2], scalar=cw(0),
                                                                              in1=yt[:, 2:W], op0=ALU.mult, op1=ALU.add)),
                                               reads=[pb, b_y, P_["prm"]], writes=[b_y])
                                            op("dve", (lambda g, pt=pt, ux=ux: g.tensor_copy(out=ux[:, 2:4], in_=pt[:, 0:2])),
                                               reads=[pb], writes=[b_uh[ch]])
                                            op("act", (lambda g, yt=yt, ux=ux, cw=cw, cb=cb:
                                                       g.activation(out=yt[:, 0:2], in_=ux[:, 2:4], func=AF.Identity,
                                                                    scale=cw(2), bias=cb)),
                                               reads=[b_uh[ch], P_["prm"]], writes=[b_yb])
                                            op("dve", (lambda g, yt=yt, ux=ux, cw=cw:
                                                       g.scalar_tensor_tensor(out=yt[:, 0:2], in0=ux[:, 1:3], scalar=cw(1),
                                                                              in1=yt[:, 0:2], op0=ALU.mult, op1=ALU.add)),
                                               reads=[b_uh[ch], b_yb, P_["prm"]], writes=[b_yb])
                                            op("dve", (lambda g, yt=yt, ux=ux, cw=cw:
                                                       g.scalar_tensor_tensor(out=yt[:, 0:2], in0=ux[:, 0:2], scalar=cw(0),
                                                                              in1=yt[:, 0:2], op0=ALU.mult, op1=ALU.add)),
                                               reads=[b_uh[ch], b_yb, P_["prm"]], writes=[b_yb])
                                            op("dve", (lambda g, pt=pt, ux=ux: g.tensor_copy(out=ux[:, 0:2], in_=pt[:, W - 2:W])),
                                               reads=[pb], writes=[b_uh[ch]])
                                            res.append((yt, b_y, b_yb))
                                        (yta, b_a, b_ab), (ytb, b_b, b_bb) = res
                                        op("act", (lambda g, yta=yta:
                                                   g.activation(out=yta[:, 0:W], in_=yta[:, 0:W], func=AF.Silu)),
                                           reads=[b_a, b_ab], writes=[b_a, b_ab])
                                        op("dve", (lambda g, yta=yta, ytb=ytb, c=c, sub=sub:
                                                   g.tensor_tensor(out=gT[:, c, sub * W:(sub + 1) * W], in0=yta[:, 0:W],
                                                                   in1=ytb[:, 0:W], op=ALU.mult)),
                                           reads=[b_a, b_ab, b_b, b_bb], writes=[b_gT[c][sub]])
                            sch.barrier(skip=ring_ds + ring_st)
                        if m == NMT - 1:
                            op("dve", lambda g: g.memset(uhx[:], 0.0), writes=b_uh)
                        if "gT" in dbg_d and seq == 0:
                            for c in range(FC):
                                dbg_dump("gT", gT[:, c, :], b_gT[c], row0=c * 128, col0=m * MT)
                        p8_es = ExitStack()
                        with p8_es:
                            def g_of(kc, sub):
                                return gT[:, kc, sub * SUBW:(sub + 1) * SUBW]

                            def g_reads(kc, sub):
                                return [b_gT[kc][sub]]
                            down_proj(p8_es, w_dn_d, FC, g_of, g_reads, G2, seq,
                                      lambda sub: out_d[tok0 + sub * SUBW: tok0 + (sub + 1) * SUBW, :],
                                      lambda sub: out_d[tok0 + sub * SUBW: tok0 + (sub + 1) * SUBW, :],
                                      lambda sub: dob(seq, m, sub), lambda sub: dob(seq, m, sub), tag="8", xo=xo8)
                            sch.barrier(skip=ring_ds + ring_st)
        fin_reads = list(dram_out_b.values())
        op("sp", lambda g: g.nop(), reads=fin_reads + [Buf("dummy")], writes=[])
        sch.barrier()
        op("sp", lambda g: g.nop(), reads=[Buf("dummy2")], writes=[])
        sch.flush()
    return nc


def CST_N(NBIS):
    return 512 + NBIS


def PRM_OFF(NSEQ):
    names = (("cT", KC * NSEQ), ("b_ada", 96), ("norm1_g", 16), ("norm2_g", 16), ("v_norm_g", 8), ("q_norm_g", 1),
             ("k_norm_g", 1), ("conv_w", 3 * 88), ("conv_b", 88), ("bsp_rep", 8 * 128))
    off = {}
    o = 0
    for n, k in names:
        off[n] = (o, k)
        o += k
    off["_total"] = (o, 0)
    return off


def PRM_N(NSEQ):
    return PRM_OFF(NSEQ)["_total"][0]


def _pp(v):
    v = np.asarray(v, np.float32)
    return np.ascontiguousarray(v.reshape(-1, 128).T)


def make_consts(NBIS):
    c = np.zeros((128, CST_N(NBIS)), np.float32)
    i = np.arange(128)
    c[:, 0:128] = np.eye(128, dtype=np.float32)
    c[:, 128:256] = (i[:, None] <= i[None, :]).astype(np.float32)
    c[:, 256:384] = np.where(i[None, :] <= i[:, None], 0.0, -1e30)
    c[:, 384:512] = 1.0
    c[:, 512:512 + NBIS] = (0.5 ** (np.arange(NBIS) + 1))[None, :]
    return c


def make_params(c_rows, b_ada, norm1_g, norm2_g, v_norm_g, q_norm_g, k_norm_g, conv_w, conv_b, b_spatial):
    NSEQ = c_rows.shape[0]
    off = PRM_OFF(NSEQ)
    p = np.zeros((128, PRM_N(NSEQ)), np.float32)

    def put(name, arr):
        o, n = off[name]
        p[:, o:o + n] = np.asarray(arr, np.float32).reshape(128, n)
    cT = np.stack([_pp(c_rows[b]) for b in range(NSEQ)], axis=-1)
    put("cT", cT)
    put("b_ada", _pp(b_ada))
    put("norm1_g", _pp(norm1_g))
    put("norm2_g", _pp(norm2_g))
    put("v_norm_g", _pp(v_norm_g))
    put("q_norm_g", np.asarray(q_norm_g).reshape(128, 1))
    put("k_norm_g", np.asarray(k_norm_g).reshape(128, 1))
    cw = np.stack([_pp(conv_w[j]) for j in range(3)], axis=1)
    put("conv_w", cw)
    put("conv_b", _pp(conv_b))
    put("bsp_rep", np.broadcast_to(np.asarray(b_spatial, np.float32).reshape(1, 8 * 128), (128, 8 * 128)))
    return p


def make_in_maps(inputs, n_cores, NSEQ, NBIS, S):
    f = lambda a: np.ascontiguousarray(np.asarray(a, dtype=np.float32))
    x = f(inputs["x"])
    c = f(inputs["c"])
    cst = make_consts(NBIS)
    wspT = np.ascontiguousarray(f(inputs["w_spatial"]).transpose(2, 0, 1).reshape(128, 8 * 128))
    shared = {k: f(inputs[k]) for k in ("w_ada", "w_in", "w_proj_a", "w_proj_b", "w_out", "w_up", "w_down")}
    maps = []
    for i in range(n_cores):
        b0 = i * NSEQ
        m = dict(shared)
        m["x"] = np.ascontiguousarray(x[b0:b0 + NSEQ].reshape(NSEQ * S, D))
        m["cst"] = cst
        m["prm"] = make_params(c[b0:b0 + NSEQ], inputs["b_ada"], inputs["norm1_g"], inputs["norm2_g"],
                               inputs["v_norm_g"], inputs["q_norm_g"], inputs["k_norm_g"], inputs["conv_w"],
                               inputs["conv_b"], inputs["b_spatial"])
        m["wspT"] = wspT
        maps.append(m)
    return maps


def kernel(**inputs):
    S, NSEQ, NBIS = 2048, 2, 24
    nc = build_nc(S=S, NSEQ=NSEQ, MT=1024, NBIS=NBIS)
    maps = make_in_maps(inputs, N_CORES, NSEQ, NBIS, S)
    res = run_bass_kernel_spmd(nc, maps, core_ids=list(range(N_CORES)))
    outs = [np.asarray(r["out"], dtype=np.float32).reshape(NSEQ, S, D) for r in res.results]
    return np.concatenate(outs, axis=0)
```
